# Optimizing a Trainium2 kernel written in Bass

```python
import jax, jax.numpy as jnp
from jax import lax
import numpy as np

D_MODEL = 1024
BATCH = 2
SEQ = 8192
DEPTH = 1

HEAD_DIM = 64
N_HEADS = D_MODEL // HEAD_DIM
N_HEADS_A = N_HEADS // 4
N_HEADS_B = N_HEADS - N_HEADS_A
WIDTH_A = N_HEADS_A * HEAD_DIM
WIDTH_B = N_HEADS_B * HEAD_DIM
CHUNK = 128
BLOCK = 128
DILATED_BRANCHES = ((128, 1), (512, 4), (2048, 16))
ROPE_THETA = 10000.0
D_FF = -(-8 * D_MODEL // (3 * 256)) * 256
PLE_DIM = 256
IN_COLS = 2 * WIDTH_A + 3 * WIDTH_B
EPS = 1e-6

kernel_name = "hybrid_sgu_dilated_attn_block"


def rmsnorm(x, g):
    xf = x.astype(jnp.float32)
    y = xf * lax.rsqrt(jnp.mean(xf * xf, axis=-1, keepdims=True) + EPS)
    return (y * g.astype(jnp.float32)).astype(x.dtype)


def rope(t, pos):
    half = t.shape[-1] // 2
    inv = ROPE_THETA ** (-jnp.arange(half, dtype=jnp.float32) / half)
    ang = pos[:, None] * inv[None, :]
    cos = jnp.cos(ang)[None, :, None, :]
    sin = jnp.sin(ang)[None, :, None, :]
    t = t.astype(jnp.float32)
    t1, t2 = t[..., :half], t[..., half:]
    return jnp.concatenate([t1 * cos - t2 * sin, t1 * sin + t2 * cos], axis=-1)


def chunked_sgu(u, v, w_s, b_s, norm_g):
    b, s, _ = u.shape
    u = jax.nn.gelu(u.astype(jnp.float32))
    vf = jax.nn.gelu(v.astype(jnp.float32))
    mu = jnp.mean(vf, axis=-1, keepdims=True)
    var = jnp.mean(jnp.square(vf - mu), axis=-1, keepdims=True)
    vf = (vf - mu) * lax.rsqrt(var + EPS) * norm_g.astype(jnp.float32)
    vf = vf.reshape(b, s // CHUNK, CHUNK, N_HEADS_A, HEAD_DIM)
    causal = jnp.tril(jnp.ones((CHUNK, CHUNK), jnp.float32))
    w = w_s.astype(jnp.float32) * causal[None]
    mixed = jnp.einsum('hij,bnjhd->bnihd', w, vf) + b_s.astype(jnp.float32).T[None, None, :, :, None]
    return u * mixed.reshape(b, s, WIDTH_A)


def dilated_branch(q, k, v, window, dilation):
    b, h, s, dh = q.shape
    n_back = window // dilation
    span = dilation * BLOCK
    s_pad = -(-s // span) * span
    sub_len = s_pad // dilation
    nb = sub_len // BLOCK

    def to_sub(t):
        t = jnp.pad(t, ((0, 0), (0, 0), (0, s_pad - s), (0, 0)))
        t = t.reshape(b, h, sub_len, dilation, dh)
        t = jnp.swapaxes(t, 2, 3)
        return t.reshape(b, h, dilation, nb, BLOCK, dh)

    qb, kb, vb = to_sub(q), to_sub(k), to_sub(v)
    shift = ((0, 0), (0, 0), (0, 0), (1, 0), (0, 0), (0, 0))
    kw = jnp.concatenate([jnp.pad(kb[:, :, :, :-1], shift), kb], axis=4)
    vw = jnp.concatenate([jnp.pad(vb[:, :, :, :-1], shift), vb], axis=4)
    scores = jnp.einsum('bhrnqd,bhrnkd->bhrnqk', qb, kw) * (dh ** -0.5)
    qi = jnp.arange(BLOCK)[:, None]
    kc = jnp.arange(2 * BLOCK)[None, :]
    dist = BLOCK + qi - kc
    band = (dist >= 0) & (dist <= n_back)
    blk = jnp.arange(nb)[:, None, None]
    valid = band[None] & ((blk > 0) | (kc[None] >= BLOCK))
    scores = jnp.where(valid, scores, -jnp.inf)
    m = jnp.max(scores, axis=-1, keepdims=True)
    pr = jnp.exp(scores - m)
    l = jnp.sum(pr, axis=-1, keepdims=True)
    o = jnp.einsum('bhrnqk,bhrnkd->bhrnqd', pr, vw) / l
    lse = (m + jnp.log(l))[..., 0]
    o = jnp.swapaxes(o.reshape(b, h, dilation, sub_len, dh), 2, 3).reshape(b, h, s_pad, dh)[:, :, :s]
    lse = jnp.swapaxes(lse.reshape(b, h, dilation, sub_len), 2, 3).reshape(b, h, s_pad)[:, :, :s]
    return o, lse


def dilated_mixture_attention(q, k, v):
    outs, lses = [], []
    for window, dilation in DILATED_BRANCHES:
        o, lse = dilated_branch(q, k, v, window, dilation)
        outs.append(o)
        lses.append(lse)
    o = jnp.stack(outs, axis=0)
    wts = jax.nn.softmax(jnp.stack(lses, axis=0), axis=0)
    return jnp.sum(wts[..., None] * o, axis=0)


def setup_inputs(seed: int = 0) -> dict:
    key = jax.random.key(seed)
    ks = jax.random.split(key, 20)
    f32 = jnp.float32

    def nrm(k, shape, fan_in):
        return jax.random.normal(k, shape, f32) * (fan_in ** -0.5)

    def gain(k, shape):
        return 1.0 + 0.05 * jax.random.normal(k, shape, f32)

    return {
        "x": jax.random.normal(ks[0], (BATCH, SEQ, D_MODEL), f32),
        "p": jax.random.normal(ks[1], (DEPTH, BATCH, SEQ, PLE_DIM), f32),
        "mix_norm_g": gain(ks[2], (DEPTH, D_MODEL)),
        "w_in": nrm(ks[3], (DEPTH, D_MODEL, IN_COLS), D_MODEL),
        "sgu_w": nrm(ks[4], (DEPTH, N_HEADS_A, CHUNK, CHUNK), CHUNK),
        "sgu_b": 1.0 + 0.1 * jax.random.normal(ks[5], (DEPTH, N_HEADS_A, CHUNK), f32),
        "sgu_norm_g": gain(ks[6], (DEPTH, WIDTH_A)),
        "out_norm_a": gain(ks[7], (DEPTH, WIDTH_A)),
        "out_norm_b": gain(ks[8], (DEPTH, WIDTH_B)),
        "w_out": nrm(ks[9], (DEPTH, D_MODEL, D_MODEL), D_MODEL),
        "ffn_norm_g": gain(ks[10], (DEPTH, D_MODEL)),
        "w_gate": nrm(ks[11], (DEPTH, D_MODEL, D_FF), D_MODEL),
        "w_up": nrm(ks[12], (DEPTH, D_MODEL, D_FF), D_MODEL),
        "w_down": nrm(ks[13], (DEPTH, D_FF, D_MODEL), D_FF),
        "ple_norm_g": gain(ks[14], (DEPTH, D_MODEL)),
        "w_ple_gate": nrm(ks[15], (DEPTH, D_MODEL, D_MODEL), D_MODEL),
        "w_ple_proj": nrm(ks[16], (DEPTH, PLE_DIM, D_MODEL), PLE_DIM),
        "final_norm_g": gain(ks[17], (D_MODEL,)),
    }


def reference(x, p, mix_norm_g, w_in, sgu_w, sgu_b, sgu_norm_g, out_norm_a, out_norm_b,
              w_out, ffn_norm_g, w_gate, w_up, w_down, ple_norm_g, w_ple_gate,
              w_ple_proj, final_norm_g):
    b, s, _ = x.shape
    pos = jnp.arange(s, dtype=jnp.float32)
    h = x
    for i in range(DEPTH):
        hn = rmsnorm(h, mix_norm_g[i])
        proj = hn @ w_in[i]
        u_a = proj[..., :WIDTH_A]
        v_a = proj[..., WIDTH_A:2 * WIDTH_A]
        qkv = proj[..., 2 * WIDTH_A:].reshape(b, s, 3, N_HEADS_B, HEAD_DIM)
        y_a = chunked_sgu(u_a, v_a, sgu_w[i], sgu_b[i], sgu_norm_g[i])
        q = jnp.transpose(rope(qkv[:, :, 0], pos), (0, 2, 1, 3))
        k = jnp.transpose(rope(qkv[:, :, 1], pos), (0, 2, 1, 3))
        v = jnp.transpose(qkv[:, :, 2].astype(jnp.float32), (0, 2, 1, 3))
        y_b = dilated_mixture_attention(q, k, v)
        y_b = jnp.transpose(y_b, (0, 2, 1, 3)).reshape(b, s, WIDTH_B)
        y = jnp.concatenate([rmsnorm(y_a, out_norm_a[i]), rmsnorm(y_b, out_norm_b[i])], axis=-1)
        h = h + (y.astype(h.dtype) @ w_out[i])
        hn = rmsnorm(h, ffn_norm_g[i])
        h = h + (jax.nn.silu(hn @ w_gate[i]) * (hn @ w_up[i])) @ w_down[i]
        gate = jax.nn.sigmoid(rmsnorm(h, ple_norm_g[i]) @ w_ple_gate[i])
        h = h + gate * (p[i] @ w_ple_proj[i])
    return rmsnorm(h, final_norm_g)
```

```python
import numpy as np
import concourse.bass as bass
import concourse.mybir as mybir
from concourse.bass_utils import run_bass_kernel_spmd

F32 = mybir.dt.float32
BF16 = mybir.dt.bfloat16
U8 = mybir.dt.uint8
AF = mybir.ActivationFunctionType
ALU = mybir.AluOpType
AX = mybir.AxisListType

D = 1024
NT = 2048
NA = 4096
DFF = 2816
EPS = 1e-6
NCORES = 8

ENGS = ["pe", "act", "dve", "pool", "sp"]


class Op:
    __slots__ = ("eng", "fn", "deps", "sig", "sigval", "dma", "dma_val", "idx")

    def __init__(self, eng, fn, dma):
        self.eng = eng
        self.fn = fn
        self.dma = dma
        self.deps = []
        self.sig = False
        self.sigval = 0
        self.dma_val = 0
        self.idx = 0


class Sched:
    def __init__(self):
        self.ops = {e: [] for e in ENGS}
        self.lastw = {}
        self.readers = {}
        self.dma_cnt = {}
        self.dma_last = {}
        self.pending = {e: [] for e in ENGS}

    def add(self, eng, fn, r=(), w=(), dma=None):
        op = Op(eng, fn, dma)
        deps = set(self.pending[eng])
        self.pending[eng] = []
        for k in r:
            d = self.lastw.get(k)
            if d is not None:
                deps.add(d)
        for k in w:
            d = self.lastw.get(k)
            if d is not None:
                deps.add(d)
            for rd in self.readers.get(k, ()):
                deps.add(rd)
        best = {}
        for d in deps:
            if d is op:
                continue
            if d.dma is None:
                if d.eng == "pe" and eng == "pe" and dma is None:
                    continue
                key = ("eng", d.eng)
                if key not in best or best[key].idx < d.idx:
                    best[key] = d
            else:
                key = ("dma", d.dma)
                if key not in best or best[key].dma_val < d.dma_val:
                    best[key] = d
        op.deps = list(best.values())
        for k in w:
            self.lastw[k] = op
            self.readers[k] = []
        for k in r:
            if k not in w:
                self.readers.setdefault(k, []).append(op)
        if dma is not None:
            c = self.dma_cnt.get(dma, 0) + 1
            self.dma_cnt[dma] = c
            op.dma_val = 16 * c
            self.dma_last[dma] = op
        op.idx = len(self.ops[eng])
        self.ops[eng].append(op)
        return op

    def barrier(self):
        lst = []
        for e in ENGS:
            for op in reversed(self.ops[e]):
                if op.dma is None:
                    lst.append(op)
                    break
        lst.extend(self.dma_last.values())
        for e in ENGS:
            self.pending[e] = list(lst)
        self.lastw = {}
        self.readers = {}

    def finalize(self):
        for e in ENGS:
            for op in self.ops[e]:
                for d in op.deps:
                    if d.dma is None:
                        d.sig = True
        for e in ENGS:
            c = 0
            for op in self.ops[e]:
                if op.sig:
                    c += 1
                    op.sigval = c

    def emit(self, ename, eng, sems):
        waited = {}
        for op in self.ops[ename]:
            for d in op.deps:
                if d.dma is not None:
                    key = ("dma", d.dma)
                    val = d.dma_val
                else:
                    key = ("eng", d.eng)
                    val = d.sigval
                if waited.get(key, 0) < val:
                    eng.wait_ge(sems[key], val)
                    waited[key] = val
            ins = op.fn(eng)
            if op.dma is not None:
                ins.then_inc(sems[("dma", op.dma)], 16)
            elif op.sig:
                ins.then_inc(sems[("eng", ename)], 1)


def attention_jobs():
    jobs = []
    vt = 0
    for kb in range(-1, 16):
        ks = slice(NT + kb * 128, NT + kb * 128 + 128, 1)
        if kb == -1:
            jobs.append((ks, vt, (0, 128, 1), "ph"))
        elif kb == 15:
            jobs.append((ks, vt, (15 * 128, 128, 1), "c"))
        else:
            jobs.append((ks, vt, (kb * 128, 256, 1), "cp"))
        vt += 1
    for r in range(4):
        for n in range(-1, 4):
            k0 = NT + n * 512 + r
            ks = slice(k0, k0 + 127 * 4 + 1, 4)
            if n == -1:
                jobs.append((ks, vt, (r, 128, 4), "ph"))
            elif n == 3:
                jobs.append((ks, vt, (1536 + r, 128, 4), "c"))
            else:
                jobs.append((ks, vt, (n * 512 + r, 256, 4), "cp"))
            vt += 1
    for r in range(16):
        for n in (-1, 0):
            k0 = (n + 1) * NT + r
            ks = slice(k0, k0 + 127 * 16 + 1, 16)
            jobs.append((ks, vt, (r, 128, 16), "ph" if n == -1 else "c"))
            vt += 1
    assert vt == 69
    return jobs


def out_pieces(q0, n, st):
    pieces = []
    i = 0
    while i < n:
        col = q0 + i * st
        bank_end = (col // 512 + 1) * 512
        cnt = min(n - i, (bank_end - col + st - 1) // st)
        pieces.append((i, cnt, col))
        i += cnt
    return pieces


def build_nc(debug=None, stop=None, npairs=6, nhalves=2):
    nc = bass.Bass("TRN2", target_bir_lowering=False)
    dr = {}

    def din(name, shape):
        dr[name] = nc.dram_tensor(name, list(shape), F32, kind="ExternalInput").ap()
        return dr[name]

    xall = din("xall", [D, NA])
    pT = din("pT", [256, NT])
    win = din("win", [22, 128, 8, 128])
    wout = din("wout", [8, 128, 8, 128])
    wgate = din("wgate", [22, 128, 8, 128])
    wup = din("wup", [22, 128, 8, 128])
    wdown = din("wdown", [8, 128, 22, 128])
    wpg = din("wpg", [8, 128, 8, 128])
    wpp = din("wpp", [128, 2, 1024])
    gains = din("gains", [128, 40])
    sgug = din("sgug", [128, 256])
    sguw = din("sguw", [128, 512])
    sgub = din("sgub", [1, 512])
    cst = din("cst", [128, 896])
    rope = din("rope", [128, 2, NA])
    outT = nc.dram_tensor("outT", [D, NT], F32, kind="ExternalOutput").ap()
    dbg = None
    if debug is not None:
        dbg = nc.dram_tensor("dbg", [128, debug[1]], F32, kind="ExternalOutput").ap()

    S = Sched()
    from contextlib import ExitStack
    es = ExitStack()
    TOTAL = 206 * 1024
    arena = es.enter_context(nc.sbuf_tensor("arena", [128, TOTAL], U8))
    psall = es.enter_context(nc.psum_tensor("psall", [128, 4096], F32))

    def bank(i):
        return psall[:, i * 512:(i + 1) * 512]

    def bankbf(i):
        return psall[:, i * 512:(i + 1) * 512].bitcast(BF16)

    cur = [0]

    def alloc(nbytes, at=None):
        if at is not None:
            return at
        off = cur[0]
        cur[0] = off + ((nbytes + 63) // 64) * 64
        assert cur[0] <= TOTAL, (cur[0], TOTAL)
        return off

    def view(off, dt, n, inner=None):
        sz = 2 if dt == BF16 else 4
        v = arena[:, off:off + n * sz].bitcast(dt)
        if inner is not None:
            v = v.rearrange("p (a b) -> p a b", b=inner)
        return v

    o_cstb = alloc(896 * 2); cstb = view(o_cstb, BF16, 896)
    ident = cstb[:, 0:128]; rotm = cstb[:, 128:256]
    MASK = cstb[:, 256:640]
    ones_bf = cstb[:, 640:768]; zeros_bf = cstb[:, 768:896]
    o_gn = alloc(40 * 4); gn = view(o_gn, F32, 40)
    o_sgug = alloc(256 * 4); sgug_sb = view(o_sgug, F32, 256)
    o_sguw = alloc(512 * 2); sguw_sb = view(o_sguw, BF16, 512)
    o_wm = alloc(512 * 2); wm = view(o_wm, BF16, 512)
    o_sgub = alloc(512 * 2); sgub_sb = view(o_sgub, BF16, 512)
    o_rope = alloc(2 * NA * 2); rope_sb = view(o_rope, BF16, 2 * NA, NA)
    cosT = rope_sb[:, 0, :]; sinT = rope_sb[:, 1, :]
    o_yT = alloc(8 * NT * 2); yT = view(o_yT, BF16, 8 * NT, NT)
    NW8 = 6
    o_w8 = [alloc(8 * 128 * 2) for _ in range(NW8)]
    w8 = [view(o, BF16, 1024, 128) for o in o_w8]
    NPT = 5
    o_pt = [alloc(512 * 2) for _ in range(NPT)]
    PT = [view(o, BF16, 512) for o in o_pt]
    big0 = cur[0]
    o_hn = alloc(8 * NA * 2); hnT = view(o_hn, BF16, 8 * NA, NA)
    X0 = cur[0]
    cur[0] = X0
    o_sqAB = [alloc(8 * 512 * 2) for _ in range(2)]
    sqAB = [view(o, BF16, 4096, 512) for o in o_sqAB]
    o_rs = [alloc(512 * 4) for _ in range(2)]
    rstd = [view(o, F32, 512) for o in o_rs]
    endA1 = cur[0]
    cur[0] = X0
    o_uT = alloc(2 * NT * 2); uT = view(o_uT, BF16, 2 * NT, NT)
    o_vgall = alloc(16 * 256 * 4); vgall = view(o_vgall, F32, 4096, 256)
    o_sqall = alloc(16 * 256 * 4); sqall = view(o_sqall, F32, 4096, 256)
    o_vfall = alloc(16 * 256 * 2); vfall = view(o_vfall, BF16, 4096, 256)
    o_sst = alloc(96 * 4); sst = view(o_sst, F32, 96)
    o_eps = alloc(64); epsc = view(o_eps, F32, 16)
    endSGU = cur[0]
    cur[0] = X0
    o_qA = alloc(NT * 2); qA = view(o_qA, BF16, NT)
    o_qB = alloc(NT * 2); qB = view(o_qB, BF16, NT)
    o_kT = alloc(NA * 2); kT = view(o_kT, BF16, NA)
    o_vr = alloc(NA * 2); vraw = view(o_vr, BF16, NA)
    Tsb = view(o_vr, F32, NT)
    o_va = alloc(69 * 192 * 2); vaug = view(o_va, BF16, 69 * 192, 192)
    o_raw = [alloc(512 * 2) for _ in range(2)]; raw = [view(o, BF16, 512) for o in o_raw]
    o_t1 = [alloc(512 * 4) for _ in range(2)]; t1 = [view(o, F32, 512) for o in o_t1]
    o_t2 = [alloc(512 * 4) for _ in range(2)]; t2 = [view(o, F32, 512) for o in o_t2]
    o_rec = [alloc(512 * 4) for _ in range(2)]; rec = [view(o, F32, 512) for o in o_rec]
    endATT = cur[0]
    cur[0] = big0
    HT = 1024
    o_hT = alloc(8 * HT * 4); hT = view(o_hT, F32, 8 * HT, HT)
    o_hnb = alloc(8 * HT * 2); hnb = view(o_hnb, BF16, 8 * HT, HT)
    o_aT = alloc(22 * HT * 2); aT = view(o_aT, BF16, 22 * HT, HT)
    sq2 = view(o_rope, BF16, 4096, 512)
    o_rs2 = [alloc(512 * 4) for _ in range(4)]; rstd2 = [view(o, F32, 512) for o in o_rs2]
    o_w22h = [alloc(11 * 128 * 2) for _ in range(4)]; w22h = [view(o, BF16, 11 * 128, 128) for o in o_w22h]
    o_wpp = alloc(2 * 1024 * 2); wpp_sb = view(o_wpp, BF16, 2048, 1024)
    o_pth = alloc(2 * HT * 2); pTh = view(o_pth, BF16, 2 * HT, HT)
    o_tmp = [alloc(512 * 4) for _ in range(2)]; tmpf = [view(o, F32, 512) for o in o_tmp]
    endPOST = cur[0]
    assert max(endA1, endSGU, endATT, endPOST) <= TOTAL, (endA1, endSGU, endATT, endPOST, TOTAL)
    cur[0] = max(endA1, endSGU, endATT, endPOST)

    w8_ctr = [0]

    def load_w8(src_ap):
        s = w8_ctr[0] % NW8
        w8_ctr[0] += 1
        S.add("pool", lambda e, s=s, src_ap=src_ap: e.dma_start(out=w8[s], in_=src_ap),
              w=[("w8", s)], dma="w8_%d" % s)
        return s

    bank_ctr = [0]
    gen_banks = [0, 1, 2, 3, 4, 5, 6, 7]

    def next_bank():
        b = gen_banks[bank_ctr[0] % len(gen_banks)]
        bank_ctr[0] += 1
        return b

    def mm_group(b, outap, pairs, extra_r=()):
        n = len(pairs)

        def fn(e):
            ins = None
            for i, (l, r_) in enumerate(pairs):
                ins = e.matmul(outap, l, r_, start=(i == 0), stop=(i == n - 1))
            return ins
        return fn

    def norm_s1(src_fn, src_keys, groups, tb, sqb, sqk, rsb, slot):
        st = []
        for gi, (chunks, div) in enumerate(groups):
            for c in chunks:
                S.add("act", lambda e, c=c, tb=tb: e.activation(sqb[:, c, :], src_fn(c, tb), AF.Square),
                      r=list(src_keys(c, tb)), w=[(sqk, c)])
            b = next_bank()
            S.add("pe", mm_group(b, bank(b), [(ones_bf, sqb[:, c, :]) for c in chunks]),
                  r=[(sqk, c) for c in chunks] + ["cst"], w=[("ps", b)])
            si = (slot + gi) % len(rsb)
            rr = rsb[si]
            rk = ("rs", id(rsb), si)
            S.add("dve", lambda e, b=b, rr=rr, div=div: e.tensor_scalar(rr, bank(b), 1.0 / div, EPS, ALU.mult, ALU.add),
                  r=[("ps", b)], w=[rk])
            st.append((chunks, rr, rk))
        return st

    def norm_s2(st, src_fn, src_keys, gcol, dst_fn, dst_keys, tb):
        for (chunks, rr, rk) in st:
            S.add("act", lambda e, rr=rr: e.activation(rr, rr, AF.Ln), r=[rk], w=[rk])
            S.add("act", lambda e, rr=rr: e.activation(rr, rr, AF.Exp, scale=-0.5), r=[rk], w=[rk])
            for c in chunks:
                S.add("dve", lambda e, c=c, tb=tb, rr=rr: e.scalar_tensor_tensor(
                    dst_fn(c, tb), src_fn(c, tb), gn[:, gcol + c:gcol + c + 1], rr, ALU.mult, ALU.mult),
                    r=list(src_keys(c, tb)) + [rk, "gn"], w=list(dst_keys(c, tb)))

    def rms_norm(src_fn, src_keys, gcol, dst_fn, dst_keys, groups, tbs, sqb, sqk, rsb):
        for tb in tbs:
            st = norm_s1(src_fn, src_keys, groups, tb, sqb, sqk, rsb, (tb * len(groups)) % len(rsb))
            norm_s2(st, src_fn, src_keys, gcol, dst_fn, dst_keys, tb)

    S.add("pool", lambda e: e.dma_start(out=cstb, in_=cst), w=["cst"], dma="c0")
    S.add("sp", lambda e: e.dma_start(out=gn, in_=gains), w=["gn"], dma="c4")
    S.add("pool", lambda e: e.dma_start(out=sguw_sb, in_=sguw), w=["sguw"], dma="c2")
    S.add("pool", lambda e: e.dma_start(out=sgub_sb[0:1, :], in_=sgub), w=["sgub"], dma="c3")
    S.add("sp", lambda e: e.dma_start(out=sgug_sb, in_=sgug), w=["sgug"], dma="c5")

    xv = xall.rearrange("(c p) t -> p c t", p=128)
    for blk in range(8):
        S.add("pool", lambda e, blk=blk: e.dma_start(out=hnT[:, :, blk * 512:(blk + 1) * 512], in_=xv[:, :, blk * 512:(blk + 1) * 512]),
              w=[("hnT", blk)], dma="xc%d" % blk)
    su = [load_w8(win[oc]) for oc in (0, 1)]
    sv = [load_w8(win[oc]) for oc in (2, 3)]
    S.add("pool", lambda e: e.dma_start(out=rope_sb, in_=rope), w=["rope"], dma="c1")
    a1_state = {}

    def a1_args(blk):
        return (lambda c, tb, blk=blk: hnT[:, c, blk * 512:(blk + 1) * 512], lambda c, tb, blk=blk: [("hnT", blk)])
    for blk in range(9):
        if blk < 8:
            sf, sk = a1_args(blk)
            a1_state[blk] = norm_s1(sf, sk, [(list(range(8)), 1024.0)], 0, sqAB[blk % 2], "sqc%d" % (blk % 2), rstd, blk % 2)
        if blk >= 1:
            sf, sk = a1_args(blk - 1)
            norm_s2(a1_state[blk - 1], sf, sk, 0, sf, sk, 0)
    S.barrier()

    RUN_A2 = stop not in ("A1",)
    RUN_A3 = stop not in ("A1", "A2")
    RUN_POST = stop not in ("A1", "A2", "A3")
    for h in range(4):
        S.add("dve", lambda e, h=h: e.tensor_tensor(wm[:, h * 128:(h + 1) * 128], sguw_sb[:, h * 128:(h + 1) * 128],
                                                    MASK[:, 0:128], ALU.mult), w=[("wm", h)])
    NTT = 16 if RUN_A2 else 0
    for tt in range(NTT):
        b = next_bank()

        def fnv(e, b=b, tt=tt):
            ins = None
            for vc in range(2):
                for c in range(8):
                    ins = e.matmul(bank(b)[:, vc * 128:(vc + 1) * 128], hnT[:, c, NT + tt * 128:NT + (tt + 1) * 128],
                                   w8[sv[vc]][:, c, :], start=(c == 0), stop=(c == 7))
            return ins
        S.add("pe", fnv, r=[("w8", sv[0]), ("w8", sv[1])], w=[("ps", b)])
        S.add("act", lambda e, b=b, tt=tt: e.activation(vgall[:, tt, :], bank(b)[:, 0:256], AF.Gelu_apprx_tanh),
              r=[("ps", b)], w=[("vg", tt)])
    for uc in range(2 if RUN_A2 else 0):
        for tb in range(4):
            b = next_bank()
            S.add("pe", mm_group(b, bank(b), [(w8[su[uc]][:, c, :], hnT[:, c, NT + tb * 512:NT + (tb + 1) * 512]) for c in range(8)]),
                  r=[("w8", su[uc])], w=[("ps", b)])
            S.add("act", lambda e, b=b, uc=uc, tb=tb: e.activation(uT[:, uc, tb * 512:(tb + 1) * 512], bank(b), AF.Gelu_apprx_tanh),
                  r=[("ps", b)], w=[("uT", uc, tb)])
    pre_w = {}
    if RUN_A2:
        S.add("pool", lambda e: e.memset(epsc, EPS), w=["epsc"])
        vgk = [("vg", tt) for tt in range(16)]
        vg_flat = vgall.rearrange("p t f -> p (t f)")
        sq_flat = sqall.rearrange("p t f -> p (t f)")
        S.add("act", lambda e: e.activation(sq_flat, vg_flat, AF.Square), r=vgk, w=["sqall"])
        S.add("dve", lambda e: e.tensor_reduce(sst[:, 0:16], vgall, AX.X, ALU.add), r=vgk, w=["s0"])
        S.add("dve", lambda e: e.tensor_reduce(sst[:, 32:48], sqall, AX.X, ALU.add), r=["sqall"], w=["s2"])
        S.add("dve", lambda e: e.tensor_scalar(sst[:, 16:32], sst[:, 0:16], 1.0 / 256.0, None, ALU.mult), r=["s0"], w=["s1"])
        S.add("dve", lambda e: e.tensor_tensor(sst[:, 48:64], sst[:, 16:32], sst[:, 16:32], ALU.mult), r=["s1"], w=["s3"])
        S.add("dve", lambda e: e.scalar_tensor_tensor(sst[:, 64:80], sst[:, 32:48], 1.0 / 256.0, sst[:, 48:64], ALU.mult, ALU.subtract),
              r=["s2", "s3"], w=["s4"])
        S.add("act", lambda e: e.activation(sst[:, 64:80], sst[:, 64:80], AF.Ln, bias=epsc[:, 0:1]), r=["s4", "epsc"], w=["s4"])
        S.add("act", lambda e: e.activation(sst[:, 80:96], sst[:, 64:80], AF.Exp, scale=-0.5), r=["s4"], w=["s5"])
        for tt in range(16):
            S.add("dve", lambda e, tt=tt: e.tensor_scalar(vgall[:, tt, :], vgall[:, tt, :], sst[:, 16 + tt:17 + tt], sst[:, 80 + tt:81 + tt],
                                                          ALU.subtract, ALU.mult),
                  r=[("vg", tt), "s1", "s5"], w=[("vg", tt)])
        S.add("dve", lambda e: e.tensor_tensor(vfall, vgall, sgug_sb.unsqueeze(1).to_broadcast([128, 16, 256]), ALU.mult),
              r=vgk, w=["vfall"])
        pre_w[0] = (load_w8(win[4]), load_w8(win[10]), load_w8(win[16]))
    for tt in range(NTT):
        b2 = next_bank()

        def fnm(e, b2=b2, tt=tt):
            ins = None
            e.matmul(bank(b2), ones_bf[0:1, :], sgub_sb[0:1, :], start=True, stop=False)
            for h in range(4):
                pc = h // 2
                o = bank(b2)[:, h * 128:(h + 1) * 128]
                ins = e.matmul(o, vfall[:, tt, pc * 128:(pc + 1) * 128], wm[:, h * 128:(h + 1) * 128], start=False, stop=(h == 3))
            return ins
        S.add("pe", fnm, r=["vfall"] + [("wm", h) for h in range(4)], w=[("ps", b2)])
        for hh in range(2):
            ps_ = slice(hh * 64, hh * 64 + 64)
            src_ = bank(b2)[ps_, :].rearrange("p (c x) -> p c x", x=256)[:, :, hh * 128:(hh + 1) * 128]
            S.add("dve", lambda e, ps_=ps_, src_=src_, tt=tt: e.tensor_tensor(
                yT[ps_, 0:2, tt * 128:(tt + 1) * 128], src_, uT[ps_, :, tt * 128:(tt + 1) * 128], ALU.mult),
                r=[("ps", b2), ("uT", 0, tt // 4), ("uT", 1, tt // 4)], w=[("yT", tt, hh)])
    S.barrier()

    jobs = attention_jobs()
    vt_slices = [None] * 69
    for (ks, vt, _q, _m) in jobs:
        vt_slices[vt] = ks
    S.add("pool", lambda e: e.memset(vaug[:, :, 64:128], 1.0), w=["vaug_ones"])
    S.add("pool", lambda e: e.memset(qA[64:128, :], 0.0), w=["qAz"])
    S.add("pool", lambda e: e.memset(qB[0:64, :], 0.0), w=["qBz"])
    mask_ap = {"cp": MASK[:, 0:256], "c": MASK[:, 0:128], "ph": MASK[:, 256:384]}
    PBANKS = [0, 1, 2, 3, 4, 5]
    pb_ctr = [0]

    def nextpb():
        b = PBANKS[pb_ctr[0] % len(PBANKS)]
        pb_ctr[0] += 1
        return b

    rr_ctr = [0]
    groups = []
    curg, curn = [], 0
    for jb in jobs:
        n = jb[2][1]
        if curn + n > 512:
            groups.append(curg)
            curg, curn = [], 0
        curg.append(jb)
        curn += n
    if curg:
        groups.append(curg)
    mbuf = {}
    msrc = {"cp": (0, 256), "c": (0, 128), "ph": (256, 128)}
    for grp in groups:
        sig = tuple(jb[3] for jb in grp)
        if sig in mbuf:
            continue
        mb_ = view(alloc(512 * 2), BF16, 512)
        mbuf[sig] = mb_
        o = 0
        for mk in sig:
            m0, mn = msrc[mk]
            S.add("dve", lambda e, mb_=mb_, o=o, m0=m0, mn=mn: e.tensor_copy(mb_[:, o:o + mn], MASK[:, m0:m0 + mn]), w=["mbuf"])
            o += mn
    deferred = []

    def drain(n):
        for _ in range(n):
            if deferred:
                a_ = deferred.pop(0)
                S.add(a_[0], a_[1], r=a_[2], w=a_[3])
    for c in range(npairs if RUN_A3 else 0):
        if c in pre_w:
            sq_, sk_, sv_ = pre_w[c]
        else:
            sq_ = load_w8(win[4 + c])
            sk_ = load_w8(win[10 + c])
            sv_ = load_w8(win[16 + c])
        blocks = [("q", sq_, tb, NT + tb * 512) for tb in range(4)] + [("k", sk_, tb, tb * 512) for tb in range(8)]
        pbank = {}

        def add_proj(i):
            which, slot, tb, t0_ = blocks[i]
            tok = slice(t0_, t0_ + 512)
            b = nextpb()
            pbank[i] = b
            S.add("pe", mm_group(b, bank(b), [(w8[slot][:, kc, :], hnT[:, kc, tok]) for kc in range(8)]),
                  r=[("w8", slot)], w=[("ps", b)])
        add_proj(0)
        for i in range(len(blocks)):
            if i + 1 < len(blocks):
                add_proj(i + 1)
            which, slot, tb, t0_ = blocks[i]
            tok = slice(t0_, t0_ + 512)
            b = pbank[i]
            rs_ = rr_ctr[0] % 2
            rr_ctr[0] += 1
            S.add("act", lambda e, b=b, rs_=rs_: e.activation(raw[rs_], bank(b), AF.Copy),
                  r=[("ps", b)], w=[("raw", rs_)])
            b2 = nextpb()
            S.add("pe", lambda e, b2=b2, rs_=rs_: e.matmul(bank(b2), rotm, raw[rs_], start=True, stop=True),
                  r=[("raw", rs_)], w=[("ps", b2)])
            S.add("dve", lambda e, b2=b2, rs_=rs_, tok=tok: e.tensor_tensor(t1[rs_], bank(b2), sinT[:, tok], ALU.mult),
                  r=[("ps", b2)], w=[("t1", rs_)])
            S.add("pool", lambda e, rs_=rs_, tok=tok: e.tensor_tensor(t2[rs_], raw[rs_], cosT[:, tok], ALU.mult),
                  r=[("raw", rs_)], w=[("t2", rs_)])
            if which == "q":
                qs = slice(tb * 512, (tb + 1) * 512)
                S.add("dve", lambda e, rs_=rs_, qs=qs: e.tensor_tensor(qA[0:64, qs], t1[rs_][0:64, :], t2[rs_][0:64, :], ALU.add),
                      r=[("t1", rs_), ("t2", rs_)], w=[("qA", tb)])
                S.add("dve", lambda e, rs_=rs_, qs=qs: e.tensor_tensor(qB[64:128, qs], t1[rs_][64:128, :], t2[rs_][64:128, :], ALU.add),
                      r=[("t1", rs_), ("t2", rs_)], w=[("qB", tb)])
            else:
                S.add("dve", lambda e, rs_=rs_, tok=tok: e.tensor_tensor(kT[:, tok], t1[rs_], t2[rs_], ALU.add),
                      r=[("t1", rs_), ("t2", rs_)], w=[("kT", tb)])
            drain(1)
        for tb in range(8):
            tok = slice(tb * 512, (tb + 1) * 512)
            b = nextpb()
            S.add("pe", mm_group(b, bank(b), [(w8[sv_][:, kc, :], hnT[:, kc, tok]) for kc in range(8)]),
                  r=[("w8", sv_)], w=[("ps", b)])
            S.add("act", lambda e, b=b, tok=tok: e.activation(vraw[:, tok], bank(b), AF.Copy),
                  r=[("ps", b)], w=[("vraw", tb)])
        vraw_keys = [("vraw", tb) for tb in range(8)]
        for g0 in range(0, 69, 8):
            g1 = min(69, g0 + 8)
            b = nextpb()

            def fnt(e, b=b, g0=g0, g1=g1):
                ins = None
                for j, vt in enumerate(range(g0, g1)):
                    ins = e.transpose(bankbf(b)[:, j * 128:(j + 1) * 128], vraw[:, vt_slices[vt]], ident)
                return ins
            S.add("pe", fnt, r=vraw_keys, w=[("ps", b)])
            n = g1 - g0
            src = bankbf(b)[:, 0:n * 128].rearrange("p (t h d) -> p t h d", h=2, d=64)
            dst = vaug[:, g0:g1, :].rearrange("p t (h d) -> p t h d", d=64)[:, :, 0::2, :]
            S.add("dve", lambda e, src=src, dst=dst: e.tensor_copy(dst, src),
                  r=[("ps", b)], w=[("vaug", g0 // 8)])
        for hd in range(2):
            qpad = qA if hd == 0 else qB
            qkey = "qA" if hd == 0 else "qB"
            osl = slice(0, 64) if hd == 0 else slice(64, 128)
            dsl = slice(64, 128) if hd == 0 else slice(0, 64)
            vcols = slice(0, 128) if hd == 0 else slice(64, 192)
            G = len(groups)
            ginfo = []
            for gi, grp in enumerate(groups):
                offs = []
                o = 0
                for jb in grp:
                    offs.append(o)
                    o += jb[2][1]
                ginfo.append((grp, offs, o, 4 + (gi % 4), gi % NPT))

            def add_S(gi):
                grp, offs, ntot, sb_, pti = ginfo[gi]

                def fns(e, grp=grp, offs=offs, sb_=sb_, qpad=qpad):
                    ins = None
                    for jb, of in zip(grp, offs):
                        ks, vt, (q0, n, st), mk = jb
                        ins = e.matmul(bank(sb_)[:, of:of + n], kT[:, ks], qpad[:, q0:q0 + (n - 1) * st + 1:st], start=True, stop=True)
                    return ins
                S.add("pe", fns, r=[("kT", t) for t in range(8)] + [(qkey, t) for t in range(4)] + ["qAz", "qBz"],
                      w=[("ps", sb_)])
                S.add("act", lambda e, sb_=sb_, pti=pti, ntot=ntot: e.activation(PT[pti][:, 0:ntot], bank(sb_)[:, 0:ntot], AF.Exp, scale=0.125),
                      r=[("ps", sb_)], w=[("PT", pti)])
                sig = tuple(jb[3] for jb in grp)
                S.add("dve", lambda e, pti=pti, ntot=ntot, sig=sig: e.tensor_tensor(
                    PT[pti][:, 0:ntot], PT[pti][:, 0:ntot], mbuf[sig][:, 0:ntot], ALU.mult),
                    r=[("PT", pti), "mbuf"], w=[("PT", pti)])
                drain(1)

            def add_PV(gi):
                grp, offs, ntot, sb_, pti = ginfo[gi]

                def fnpv(e, grp=grp, offs=offs, pti=pti, vcols=vcols):
                    ins = None
                    for jb, of in zip(grp, offs):
                        ks, vt, (q0, n, st), mk = jb
                        for (i0, cnt, col) in out_pieces(q0, n, st):
                            ins = e.matmul(psall[:, col:col + (cnt - 1) * st + 1:st], vaug[:, vt, vcols],
                                           PT[pti][:, of + i0:of + i0 + cnt], start=False, stop=False, skip_group_check=True)
                    return ins
                banks_ = sorted({col // 512 for jb in grp for (_i0, _cnt, col) in out_pieces(*jb[2])})
                for b_ in banks_:
                    if b_ not in zeroed:
                        zeroed.add(b_)
                        S.add("pe", lambda e, b_=b_: e.matmul(bank(b_), zeros_bf, kT[:, 0:512], start=True, stop=False, skip_group_check=True),
                              r=[("kT", 0)], w=[("ps", b_)])
                S.add("pe", fnpv, r=[("PT", pti), "vaug_ones"] + [("vaug", g) for g in range(9)],
                      w=[("ps", b_) for b_ in banks_])

            zeroed = set()
            LA = 4
            for gi in range(G + LA):
                if gi < G:
                    add_S(gi)
                if gi >= LA:
                    add_PV(gi - LA)
            assert zeroed == {0, 1, 2, 3}
            vk = [("vraw", t) for t in range(8)]
            for b_ in range(4):
                cs = slice(b_ * 512, (b_ + 1) * 512)
                if b_ % 2 == 0:
                    S.add("act", lambda e, b_=b_, cs=cs: e.activation(Tsb[:, cs], bank(b_), AF.Copy),
                          r=[("ps", b_)], w=vk[2 * b_:2 * b_ + 2])
                else:
                    S.add("dve", lambda e, b_=b_, cs=cs: e.tensor_copy(Tsb[:, cs], bank(b_)),
                          r=[("ps", b_)], w=vk[2 * b_:2 * b_ + 2])
            for b_ in range(4):
                rb = b_ % 2
                cs = slice(b_ * 512, (b_ + 1) * 512)
                deferred.append(("act", lambda e, dsl=dsl, cs=cs: e.activation(Tsb[dsl, cs], Tsb[dsl, cs], AF.Ln),
                                 vk[2 * b_:2 * b_ + 2], vk[2 * b_:2 * b_ + 2]))
                deferred.append(("act", lambda e, rb=rb, osl=osl, dsl=dsl, cs=cs: e.activation(rec[rb][osl, :], Tsb[dsl, cs], AF.Exp, scale=-1.0),
                                 vk[2 * b_:2 * b_ + 2], [("rec", rb)]))
                deferred.append(("dve", lambda e, rb=rb, osl=osl, cs=cs, c=c: e.tensor_tensor(yT[osl, 2 + c, cs], Tsb[osl, cs], rec[rb][osl, :], ALU.mult),
                                 vk[2 * b_:2 * b_ + 2] + [("rec", rb)], [("yTb", c, hd, b_)]))
    drain(len(deferred))
    S.barrier()

    S.add("pool", lambda e: e.dma_start(out=wpp_sb, in_=wpp), w=["wpp"], dma="wpp")
    w22_ctr = [0]
    tmp_ctr = [0]
    pT_v = pT.rearrange("(c p) t -> p c t", p=128)
    out_v = outT.rearrange("(c p) t -> p c t", p=128)
    G1 = [(list(range(8)), 1024.0)]

    def hsrc(c, tb):
        return hT[:, c, tb * 512:(tb + 1) * 512]

    def hkeys(c, tb):
        return [("hT", c, tb)]

    def nbsrc(c, tb):
        return hnb[:, c, tb * 512:(tb + 1) * 512]

    def nbkeys(c, tb):
        return [("hnb", c, tb)]

    def ld_hT(half, tb):
        c0_ = NT + half * HT + tb * 512
        S.add("sp", lambda e, c0_=c0_, tb=tb: e.dma_start(out=hT[:, :, tb * 512:(tb + 1) * 512], in_=xv[:, :, c0_:c0_ + 512]),
              w=[("hT", o, tb) for o in range(8)], dma="hT%d" % tb)

    def ld_pT(half):
        h0 = half * HT
        S.add("pool", lambda e, h0=h0: e.dma_start(out=pTh, in_=pT_v[:, :, h0:h0 + HT]), w=["pTh"], dma="pTh")

    YG = [([0, 1], 256.0), ([2, 3, 4, 5, 6, 7], 768.0)]

    def norm_steps(kind, half, tb):
        h0 = half * HT
        if kind == "yn":
            src = lambda c, tb_, h0=h0: yT[:, c, h0 + tb_ * 512:h0 + (tb_ + 1) * 512]
            skeys = lambda c, tb_: []
            gcol, groups = 8, YG
            dst = lambda c, tb_: aT[:, c, tb_ * 512:(tb_ + 1) * 512]
            dkeys = lambda c, tb_: [("aT", c, tb_)]
        elif kind == "E":
            src, skeys, gcol, dst, dkeys, groups = hsrc, hkeys, 32, hsrc, hkeys, G1
        else:
            src, skeys, gcol, dst, dkeys, groups = hsrc, hkeys, (16 if kind == "nC" else 24), nbsrc, nbkeys, G1
        box = {}
        slot = (tb * len(groups)) % len(rstd2)

        def s1():
            box["st"] = norm_s1(src, skeys, groups, tb, sq2, "sqc2", rstd2, slot)

        def s2():
            norm_s2(box["st"], src, skeys, gcol, dst, dkeys, tb)
            if kind == "E":
                S.add("sp", lambda e, h0=h0, tb=tb: e.dma_start(out=out_v[:, :, h0 + tb * 512:h0 + (tb + 1) * 512],
                                                                 in_=hT[:, :, tb * 512:(tb + 1) * 512]),
                      r=[("hT", o, tb) for o in range(8)], dma="out")
        return s1, s2

    def steps_B(tb):
        ts_ = slice(tb * 512, (tb + 1) * 512)

        def mk(o):
            def st():
                s_ = load_w8(wout[o])
                b = next_bank()
                S.add("pe", mm_group(b, bank(b), [(w8[s_][:, kc, :], aT[:, kc, ts_]) for kc in range(8)]),
                      r=[("w8", s_)] + [("aT", kc, tb) for kc in range(8)], w=[("ps", b)])
                S.add("dve", lambda e, b=b: e.tensor_tensor(hT[:, o, ts_], hT[:, o, ts_], bank(b), ALU.add),
                      r=[("ps", b), ("hT", o, tb)], w=[("hT", o, tb)])
            return st
        return [mk(o) for o in range(8)]

    def steps_GU(tb):
        ts_ = slice(tb * 512, (tb + 1) * 512)

        def mk(f):
            def st():
                sg = load_w8(wgate[f])
                su_ = load_w8(wup[f])
                bg = next_bank()
                S.add("pe", mm_group(bg, bank(bg), [(w8[sg][:, kc, :], hnb[:, kc, ts_]) for kc in range(8)]),
                      r=[("w8", sg)] + [("hnb", kc, tb) for kc in range(8)], w=[("ps", bg)])
                bu = next_bank()
                S.add("pe", mm_group(bu, bank(bu), [(w8[su_][:, kc, :], hnb[:, kc, ts_]) for kc in range(8)]),
                      r=[("w8", su_)] + [("hnb", kc, tb) for kc in range(8)], w=[("ps", bu)])
                tsl = tmp_ctr[0] % 2
                tmp_ctr[0] += 1
                S.add("act", lambda e, bg=bg, tsl=tsl: e.activation(tmpf[tsl], bank(bg), AF.Silu),
                      r=[("ps", bg)], w=[("tmpf", tsl)])
                S.add("dve", lambda e, bu=bu, tsl=tsl: e.tensor_tensor(aT[:, f, ts_], tmpf[tsl], bank(bu), ALU.mult),
                      r=[("ps", bu), ("tmpf", tsl)], w=[("aT", f, tb)])
            return st
        return [mk(f) for f in range(22)]

    def steps_DN(tb):
        ts_ = slice(tb * 512, (tb + 1) * 512)

        def mk(o):
            def st():
                b = next_bank()
                for hf in range(2):
                    s_ = w22_ctr[0] % 4
                    w22_ctr[0] += 1
                    S.add("pool", lambda e, s_=s_, hf=hf: e.dma_start(out=w22h[s_], in_=wdown[o][:, hf * 11:(hf + 1) * 11, :]),
                          w=[("w22h", s_)], dma="w22h_%d" % s_)

                    def fn(e, s_=s_, hf=hf, b=b):
                        ins = None
                        for j in range(11):
                            f = hf * 11 + j
                            ins = e.matmul(bank(b), w22h[s_][:, j, :], aT[:, f, ts_], start=(f == 0), stop=(f == 21))
                        return ins
                    S.add("pe", fn, r=[("w22h", s_)] + [("aT", f, tb) for f in range(hf * 11, hf * 11 + 11)], w=[("ps", b)])
                S.add("dve", lambda e, b=b: e.tensor_tensor(hT[:, o, ts_], hT[:, o, ts_], bank(b), ALU.add),
                      r=[("ps", b), ("hT", o, tb)], w=[("hT", o, tb)])
            return st
        return [mk(o) for o in range(8)]

    def steps_D(tb):
        ts_ = slice(tb * 512, (tb + 1) * 512)

        def mk(o):
            def st():
                s_ = load_w8(wpg[o])
                bg = next_bank()
                S.add("pe", mm_group(bg, bank(bg), [(w8[s_][:, kc, :], hnb[:, kc, ts_]) for kc in range(8)]),
                      r=[("w8", s_)] + [("hnb", kc, tb) for kc in range(8)], w=[("ps", bg)])
                bp = next_bank()
                S.add("pe", mm_group(bp, bank(bp), [(wpp_sb[:, kc, o * 128:(o + 1) * 128], pTh[:, kc, ts_]) for kc in range(2)]),
                      r=["wpp", "pTh"], w=[("ps", bp)])
                tsl = tmp_ctr[0] % 2
                tmp_ctr[0] += 1
                S.add("act", lambda e, bg=bg, tsl=tsl: e.activation(tmpf[tsl], bank(bg), AF.Sigmoid),
                      r=[("ps", bg)], w=[("tmpf", tsl)])
                S.add("dve", lambda e, bp=bp, tsl=tsl: e.tensor_tensor(tmpf[tsl], tmpf[tsl], bank(bp), ALU.mult),
                      r=[("ps", bp), ("tmpf", tsl)], w=[("tmpf", tsl)])
                S.add("pool", lambda e, tsl=tsl: e.tensor_tensor(hT[:, o, ts_], hT[:, o, ts_], tmpf[tsl], ALU.add),
                      r=[("tmpf", tsl), ("hT", o, tb)], w=[("hT", o, tb)])
            return st
        return [mk(o) for o in range(8)]

    def run(steps, inserts=None):
        inserts = inserts or {}
        for i, st in enumerate(steps):
            st()
            for x in inserts.get(i, ()):
                x()

    if RUN_POST:
        N = norm_steps
        ld_hT(0, 0); ld_hT(0, 1); ld_pT(0)
        for tb in range(2):
            a, b_ = N("yn", 0, tb)
            a(); b_()
        for half in range(2):
            nC0 = N("nC", half, 0); nC1 = N("nC", half, 1)
            nD0 = N("nD", half, 0); nD1 = N("nD", half, 1)
            E0 = N("E", half, 0); E1 = N("E", half, 1)
            if half == 0:
                run(steps_B(0))
            run(steps_B(1), {2: [nC0[0]], 5: [nC0[1]]})
            run(steps_GU(0), {2: [nC1[0]], 5: [nC1[1]]})
            run(steps_GU(1))
            run(steps_DN(0))
            run(steps_DN(1), {2: [nD0[0]], 5: [nD0[1]]})
            if half == 0:
                yn0 = N("yn", 1, 0); yn1 = N("yn", 1, 1)
                run(steps_D(0), {1: [nD1[0]], 3: [nD1[1]], 5: [yn0[0]], 7: [yn0[1]]})
                run(steps_D(1), {1: [E0[0]], 3: [E0[1], lambda: ld_hT(1, 0)], 5: [yn1[0]], 7: [yn1[1]]})
                run(steps_B(0), {1: [E1[0]], 3: [E1[1], lambda: ld_hT(1, 1), lambda: ld_pT(1)]})
            else:
                run(steps_D(0), {2: [nD1[0]], 5: [nD1[1]]})
                run(steps_D(1), {2: [E0[0]], 5: [E0[1]]})
                E1[0](); E1[1]()
    S.barrier()
    if dbg is not None:
        dbuf = debug[0](locals())
        S.add("pool", lambda e: e.dma_start(out=dbg, in_=dbuf), dma="dbg")
        S.barrier()
    S.finalize()

    sems = {}
    for e in ENGS:
        sems[("eng", e)] = es.enter_context(nc.semaphore("s_" + e))
    for k in S.dma_cnt:
        sems[("dma", k)] = es.enter_context(nc.semaphore("d_" + k))
    with es:
        with nc.Block() as block:
            @block.tensor
            def _(t):
                S.emit("pe", t, sems)

            @block.scalar
            def _(s):
                S.emit("act", s, sems)

            @block.vector
            def _(v):
                S.emit("dve", v, sems)

            @block.gpsimd
            def _(g):
                S.emit("pool", g, sems)

            @block.sync
            def _(sy):
                S.emit("sp", sy, sems)
                for k, cnt in S.dma_cnt.items():
                    sy.wait_ge(sems[("dma", k)], 16 * cnt)
    return nc


def _tile_w(w, kc):
    K, N = w.shape
    return np.ascontiguousarray(w.reshape(kc, 128, N // 128, 128).transpose(2, 1, 0, 3))


def prep_inputs(x, p, mix_norm_g, w_in, sgu_w, sgu_b, sgu_norm_g, out_norm_a, out_norm_b,
                w_out, ffn_norm_g, w_gate, w_up, w_down, ple_norm_g, w_ple_gate,
                w_ple_proj, final_norm_g):
    f32 = np.float32
    x = np.asarray(x, f32); p = np.asarray(p, f32)
    shared = {
        "win": _tile_w(np.asarray(w_in[0], f32), 8),
        "wout": _tile_w(np.asarray(w_out[0], f32), 8),
        "wgate": _tile_w(np.asarray(w_gate[0], f32), 8),
        "wup": _tile_w(np.asarray(w_up[0], f32), 8),
        "wdown": _tile_w(np.asarray(w_down[0], f32), 22),
        "wpg": _tile_w(np.asarray(w_ple_gate[0], f32), 8),
        "wpp": np.ascontiguousarray(np.asarray(w_ple_proj[0], f32).reshape(2, 128, 1024).transpose(1, 0, 2)),
    }
    gcols = np.concatenate([
        np.asarray(mix_norm_g[0], f32).reshape(8, 128),
        np.concatenate([np.asarray(out_norm_a[0], f32), np.asarray(out_norm_b[0], f32)]).reshape(8, 128),
        np.asarray(ffn_norm_g[0], f32).reshape(8, 128),
        np.asarray(ple_norm_g[0], f32).reshape(8, 128),
        np.asarray(final_norm_g, f32).reshape(8, 128)], axis=0)
    shared["gains"] = np.ascontiguousarray(gcols.T)
    shared["sgug"] = np.ascontiguousarray(np.broadcast_to(np.asarray(sgu_norm_g[0], f32)[None, :], (128, 256)))
    shared["sguw"] = np.ascontiguousarray(np.asarray(sgu_w[0], f32).transpose(2, 0, 1).reshape(128, 512))
    shared["sgub"] = np.ascontiguousarray(np.asarray(sgu_b[0], f32).reshape(1, 512))
    ii = np.arange(128)
    ident = np.eye(128, dtype=f32)
    rotm = np.zeros((128, 128), f32)
    for m in range(128):
        if (m % 64) < 32:
            rotm[m + 32, m] = -1.0
        else:
            rotm[m - 32, m] = 1.0
    mcur = (ii[None, :] >= ii[:, None]).astype(f32)
    mprev = (ii[None, :] <= ii[:, None]).astype(f32)
    inv = (10000.0 ** (-np.arange(32, dtype=f32) / f32(32))).astype(f32)
    in_maps = []
    for core in range(NCORES):
        b, nci = core // 4, core % 4
        own = x[b, nci * NT:(nci + 1) * NT, :]
        if nci > 0:
            halo = x[b, (nci - 1) * NT:nci * NT, :]
        else:
            halo = np.zeros_like(own)
        xall = np.ascontiguousarray(np.concatenate([halo, own], axis=0).T)
        pTc = np.ascontiguousarray(p[0, b, nci * NT:(nci + 1) * NT, :].T)
        mph = mprev if nci > 0 else np.zeros_like(mprev)
        cstc = np.concatenate([ident, rotm, mcur, mprev * 0 + mprev, mph, np.ones((128, 128), f32), np.zeros((128, 128), f32)], axis=1)
        cstc[:, 384:512] = mprev
        pos = (np.arange(NA, dtype=np.int64) + (nci - 1) * NT).astype(f32)
        ang = pos[None, :] * inv[:, None]
        cs = np.cos(ang).astype(f32); sn = np.sin(ang).astype(f32)
        ropec = np.stack([np.tile(cs, (4, 1)), np.tile(sn, (4, 1))], axis=1)
        m = dict(shared)
        m["xall"] = xall
        m["pT"] = pTc
        m["cst"] = np.ascontiguousarray(cstc)
        m["rope"] = np.ascontiguousarray(ropec.astype(f32))
        in_maps.append(m)
    return in_maps


_NC_CACHE = {}


def kernel(**inputs):
    in_maps = prep_inputs(**inputs)
    if "nc" not in _NC_CACHE:
        _NC_CACHE["nc"] = build_nc()
    nc = _NC_CACHE["nc"]
    res = run_bass_kernel_spmd(nc, in_maps, core_ids=list(range(NCORES)))
    out = np.empty((2, 8192, D), np.float32)
    for core in range(NCORES):
        b, nci = core // 4, core % 4
        out[b, nci * NT:(nci + 1) * NT, :] = np.asarray(res.results[core]["outT"]).T
    return out
```

```python
import numpy as np
import concourse.bass as bass
import concourse.mybir as mybir
from concourse.bass_utils import run_bass_kernel_spmd

F32 = mybir.dt.float32
BF16 = mybir.dt.bfloat16
U8 = mybir.dt.uint8
AF = mybir.ActivationFunctionType
ALU = mybir.AluOpType
AX = mybir.AxisListType

D = 1024
NT = 2048
NA = 4096
DFF = 2816
EPS = 1e-6
NCORES = 8

ENGS = ["pe", "act", "dve", "pool", "sp"]


class Op:
    __slots__ = ("eng", "fn", "deps", "sig", "sigval", "dma", "dma_val", "idx")

    def __init__(self, eng, fn, dma):
        self.eng = eng
        self.fn = fn
        self.dma = dma
        self.deps = []
        self.sig = False
        self.sigval = 0
        self.dma_val = 0
        self.idx = 0


class Sched:
    def __init__(self):
        self.ops = {e: [] for e in ENGS}
        self.lastw = {}
        self.readers = {}
        self.dma_cnt = {}
        self.dma_last = {}
        self.pending = {e: [] for e in ENGS}

    def add(self, eng, fn, r=(), w=(), dma=None):
        op = Op(eng, fn, dma)
        deps = set(self.pending[eng])
        self.pending[eng] = []
        for k in r:
            d = self.lastw.get(k)
            if d is not None:
                deps.add(d)
        for k in w:
            d = self.lastw.get(k)
            if d is not None:
                deps.add(d)
            for rd in self.readers.get(k, ()):
                deps.add(rd)
        best = {}
        for d in deps:
            if d is op:
                continue
            if d.dma is None:
                if d.eng == "pe" and eng == "pe" and dma is None:
                    continue
                key = ("eng", d.eng)
                if key not in best or best[key].idx < d.idx:
                    best[key] = d
            else:
                key = ("dma", d.dma)
                if key not in best or best[key].dma_val < d.dma_val:
                    best[key] = d
        op.deps = list(best.values())
        for k in w:
            self.lastw[k] = op
            self.readers[k] = []
        for k in r:
            if k not in w:
                self.readers.setdefault(k, []).append(op)
        if dma is not None:
            c = self.dma_cnt.get(dma, 0) + 1
            self.dma_cnt[dma] = c
            op.dma_val = 16 * c
            self.dma_last[dma] = op
        op.idx = len(self.ops[eng])
        self.ops[eng].append(op)
        return op

    def barrier(self):
        lst = []
        for e in ENGS:
            for op in reversed(self.ops[e]):
                if op.dma is None:
                    lst.append(op)
                    break
        lst.extend(self.dma_last.values())
        for e in ENGS:
            self.pending[e] = list(lst)
        self.lastw = {}
        self.readers = {}

    def finalize(self):
        for e in ENGS:
            for op in self.ops[e]:
                for d in op.deps:
                    if d.dma is None:
                        d.sig = True
        for e in ENGS:
            c = 0
            for op in self.ops[e]:
                if op.sig:
                    c += 1
                    op.sigval = c

    def emit(self, ename, eng, sems):
        waited = {}
        for op in self.ops[ename]:
            for d in op.deps:
                if d.dma is not None:
                    key = ("dma", d.dma)
                    val = d.dma_val
                else:
                    key = ("eng", d.eng)
                    val = d.sigval
                if waited.get(key, 0) < val:
                    eng.wait_ge(sems[key], val)
                    waited[key] = val
            ins = op.fn(eng)
            if op.dma is not None:
                ins.then_inc(sems[("dma", op.dma)], 16)
            elif op.sig:
                ins.then_inc(sems[("eng", ename)], 1)


def attention_jobs():
    jobs = []
    vt = 0
    for kb in range(-1, 16):
        ks = slice(NT + kb * 128, NT + kb * 128 + 128, 1)
        if kb == -1:
            jobs.append((ks, vt, (0, 128, 1), "ph"))
        elif kb == 15:
            jobs.append((ks, vt, (15 * 128, 128, 1), "c"))
        else:
            jobs.append((ks, vt, (kb * 128, 256, 1), "cp"))
        vt += 1
    for r in range(4):
        for n in range(-1, 4):
            k0 = NT + n * 512 + r
            ks = slice(k0, k0 + 127 * 4 + 1, 4)
            if n == -1:
                jobs.append((ks, vt, (r, 128, 4), "ph"))
            elif n == 3:
                jobs.append((ks, vt, (1536 + r, 128, 4), "c"))
            else:
                jobs.append((ks, vt, (n * 512 + r, 256, 4), "cp"))
            vt += 1
    for r in range(16):
        for n in (-1, 0):
            k0 = (n + 1) * NT + r
            ks = slice(k0, k0 + 127 * 16 + 1, 16)
            jobs.append((ks, vt, (r, 128, 16), "ph" if n == -1 else "c"))
            vt += 1
    assert vt == 69
    return jobs


def out_pieces(q0, n, st):
    pieces = []
    i = 0
    while i < n:
        col = q0 + i * st
        bank_end = (col // 512 + 1) * 512
        cnt = min(n - i, (bank_end - col + st - 1) // st)
        pieces.append((i, cnt, col))
        i += cnt
    return pieces


def build_nc(debug=None, stop=None, npairs=6, nhalves=2):
    nc = bass.Bass("TRN2", target_bir_lowering=False)
    dr = {}

    def din(name, shape):
        dr[name] = nc.dram_tensor(name, list(shape), F32, kind="ExternalInput").ap()
        return dr[name]

    xall = din("xall", [D, NA])
    pT = din("pT", [256, NT])
    win = din("win", [22, 128, 8, 128])
    wout = din("wout", [8, 128, 8, 128])
    wgate = din("wgate", [22, 128, 8, 128])
    wup = din("wup", [22, 128, 8, 128])
    wdown = din("wdown", [8, 128, 22, 128])
    wpg = din("wpg", [8, 128, 8, 128])
    wpp = din("wpp", [128, 2, 1024])
    gains = din("gains", [128, 40])
    sgug = din("sgug", [128, 256])
    sguw = din("sguw", [128, 512])
    sgub = din("sgub", [1, 512])
    cst = din("cst", [128, 896])
    rope = din("rope", [128, 2, NA])
    outT = nc.dram_tensor("outT", [D, NT], F32, kind="ExternalOutput").ap()
    dbg = None
    if debug is not None:
        dbg = nc.dram_tensor("dbg", [128, debug[1]], F32, kind="ExternalOutput").ap()

    S = Sched()
    from contextlib import ExitStack
    es = ExitStack()
    TOTAL = 206 * 1024
    arena = es.enter_context(nc.sbuf_tensor("arena", [128, TOTAL], U8))
    psall = es.enter_context(nc.psum_tensor("psall", [128, 4096], F32))

    def bank(i):
        return psall[:, i * 512:(i + 1) * 512]

    def bankbf(i):
        return psall[:, i * 512:(i + 1) * 512].bitcast(BF16)

    cur = [0]

    def alloc(nbytes, at=None):
        if at is not None:
            return at
        off = cur[0]
        cur[0] = off + ((nbytes + 63) // 64) * 64
        assert cur[0] <= TOTAL, (cur[0], TOTAL)
        return off

    def view(off, dt, n, inner=None):
        sz = 2 if dt == BF16 else 4
        v = arena[:, off:off + n * sz].bitcast(dt)
        if inner is not None:
            v = v.rearrange("p (a b) -> p a b", b=inner)
        return v

    o_cstb = alloc(896 * 2); cstb = view(o_cstb, BF16, 896)
    ident = cstb[:, 0:128]; rotm = cstb[:, 128:256]
    MASK = cstb[:, 256:640]
    ones_bf = cstb[:, 640:768]; zeros_bf = cstb[:, 768:896]
    o_gn = alloc(40 * 4); gn = view(o_gn, F32, 40)
    o_sgug = alloc(256 * 4); sgug_sb = view(o_sgug, F32, 256)
    o_sguw = alloc(512 * 2); sguw_sb = view(o_sguw, BF16, 512)
    o_wm = alloc(512 * 2); wm = view(o_wm, BF16, 512)
    o_sgub = alloc(512 * 2); sgub_sb = view(o_sgub, BF16, 512)
    o_rope = alloc(2 * NA * 2); rope_sb = view(o_rope, BF16, 2 * NA, NA)
    cosT = rope_sb[:, 0, :]; sinT = rope_sb[:, 1, :]
    o_yT = alloc(8 * NT * 2); yT = view(o_yT, BF16, 8 * NT, NT)
    NW8 = 6
    o_w8 = [alloc(8 * 128 * 2) for _ in range(NW8)]
    w8 = [view(o, BF16, 1024, 128) for o in o_w8]
    NPT = 5
    o_pt = [alloc(512 * 2) for _ in range(NPT)]
    PT = [view(o, BF16, 512) for o in o_pt]
    big0 = cur[0]
    o_hn = alloc(8 * NA * 2); hnT = view(o_hn, BF16, 8 * NA, NA)
    X0 = cur[0]
    cur[0] = X0
    o_sqAB = [alloc(8 * 512 * 2) for _ in range(2)]
    sqAB = [view(o, BF16, 4096, 512) for o in o_sqAB]
    o_rs = [alloc(512 * 4) for _ in range(2)]
    rstd = [view(o, F32, 512) for o in o_rs]
    endA1 = cur[0]
    cur[0] = X0
    o_uT = alloc(2 * NT * 2); uT = view(o_uT, BF16, 2 * NT, NT)
    o_vgall = alloc(16 * 256 * 4); vgall = view(o_vgall, F32, 4096, 256)
    o_sqall = alloc(16 * 256 * 4); sqall = view(o_sqall, F32, 4096, 256)
    o_vfall = alloc(16 * 256 * 2); vfall = view(o_vfall, BF16, 4096, 256)
    o_sst = alloc(96 * 4); sst = view(o_sst, F32, 96)
    o_eps = alloc(64); epsc = view(o_eps, F32, 16)
    endSGU = cur[0]
    cur[0] = X0
    o_qA = alloc(NT * 2); qA = view(o_qA, BF16, NT)
    o_qB = alloc(NT * 2); qB = view(o_qB, BF16, NT)
    o_kT = alloc(NA * 2); kT = view(o_kT, BF16, NA)
    o_vr = alloc(NA * 2); vraw = view(o_vr, BF16, NA)
    Tsb = view(o_vr, F32, NT)
    o_va = alloc(69 * 192 * 2); vaug = view(o_va, BF16, 69 * 192, 192)
    o_raw = [alloc(512 * 2) for _ in range(2)]; raw = [view(o, BF16, 512) for o in o_raw]
    o_t1 = [alloc(512 * 4) for _ in range(2)]; t1 = [view(o, F32, 512) for o in o_t1]
    o_t2 = [alloc(512 * 4) for _ in range(2)]; t2 = [view(o, F32, 512) for o in o_t2]
    o_rec = [alloc(512 * 4) for _ in range(2)]; rec = [view(o, F32, 512) for o in o_rec]
    endATT = cur[0]
    cur[0] = big0
    HT = 1024
    o_hT = alloc(8 * HT * 4); hT = view(o_hT, F32, 8 * HT, HT)
    o_hnb = alloc(8 * HT * 2); hnb = view(o_hnb, BF16, 8 * HT, HT)
    o_aT = alloc(22 * HT * 2); aT = view(o_aT, BF16, 22 * HT, HT)
    sq2 = view(o_rope, BF16, 4096, 512)
    o_rs2 = [alloc(512 * 4) for _ in range(4)]; rstd2 = [view(o, F32, 512) for o in o_rs2]
    o_w22h = [alloc(11 * 128 * 2) for _ in range(4)]; w22h = [view(o, BF16, 11 * 128, 128) for o in o_w22h]
    o_wpp = alloc(2 * 1024 * 2); wpp_sb = view(o_wpp, BF16, 2048, 1024)
    o_pth = alloc(2 * HT * 2); pTh = view(o_pth, BF16, 2 * HT, HT)
    o_tmp = [alloc(512 * 4) for _ in range(2)]; tmpf = [view(o, F32, 512) for o in o_tmp]
    endPOST = cur[0]
    assert max(endA1, endSGU, endATT, endPOST) <= TOTAL, (endA1, endSGU, endATT, endPOST, TOTAL)
    cur[0] = max(endA1, endSGU, endATT, endPOST)

    w8_ctr = [0]

    def load_w8(src_ap):
        s = w8_ctr[0] % NW8
        w8_ctr[0] += 1
        S.add("pool", lambda e, s=s, src_ap=src_ap: e.dma_start(out=w8[s], in_=src_ap),
              w=[("w8", s)], dma="w8_%d" % s)
        return s

    bank_ctr = [0]
    gen_banks = [0, 1, 2, 3, 4, 5, 6, 7]

    def next_bank():
        b = gen_banks[bank_ctr[0] % len(gen_banks)]
        bank_ctr[0] += 1
        return b

    def mm_group(b, outap, pairs, extra_r=()):
        n = len(pairs)

        def fn(e):
            ins = None
            for i, (l, r_) in enumerate(pairs):
                ins = e.matmul(outap, l, r_, start=(i == 0), stop=(i == n - 1))
            return ins
        return fn

    def norm_s1(src_fn, src_keys, groups, tb, sqb, sqk, rsb, slot):
        st = []
        for gi, (chunks, div) in enumerate(groups):
            for c in chunks:
                S.add("act", lambda e, c=c, tb=tb: e.activation(sqb[:, c, :], src_fn(c, tb), AF.Square),
                      r=list(src_keys(c, tb)), w=[(sqk, c)])
            b = next_bank()
            S.add("pe", mm_group(b, bank(b), [(ones_bf, sqb[:, c, :]) for c in chunks]),
                  r=[(sqk, c) for c in chunks] + ["cst"], w=[("ps", b)])
            si = (slot + gi) % len(rsb)
            rr = rsb[si]
            rk = ("rs", id(rsb), si)
            S.add("dve", lambda e, b=b, rr=rr, div=div: e.tensor_scalar(rr, bank(b), 1.0 / div, EPS, ALU.mult, ALU.add),
                  r=[("ps", b)], w=[rk])
            st.append((chunks, rr, rk))
        return st

    def norm_s2(st, src_fn, src_keys, gcol, dst_fn, dst_keys, tb):
        for (chunks, rr, rk) in st:
            S.add("act", lambda e, rr=rr: e.activation(rr, rr, AF.Ln), r=[rk], w=[rk])
            S.add("act", lambda e, rr=rr: e.activation(rr, rr, AF.Exp, scale=-0.5), r=[rk], w=[rk])
            for c in chunks:
                S.add("dve", lambda e, c=c, tb=tb, rr=rr: e.scalar_tensor_tensor(
                    dst_fn(c, tb), src_fn(c, tb), gn[:, gcol + c:gcol + c + 1], rr, ALU.mult, ALU.mult),
                    r=list(src_keys(c, tb)) + [rk, "gn"], w=list(dst_keys(c, tb)))

    def rms_norm(src_fn, src_keys, gcol, dst_fn, dst_keys, groups, tbs, sqb, sqk, rsb):
        for tb in tbs:
            st = norm_s1(src_fn, src_keys, groups, tb, sqb, sqk, rsb, (tb * len(groups)) % len(rsb))
            norm_s2(st, src_fn, src_keys, gcol, dst_fn, dst_keys, tb)

    S.add("pool", lambda e: e.dma_start(out=cstb, in_=cst), w=["cst"], dma="c0")
    S.add("sp", lambda e: e.dma_start(out=gn, in_=gains), w=["gn"], dma="c4")
    S.add("pool", lambda e: e.dma_start(out=sguw_sb, in_=sguw), w=["sguw"], dma="c2")
    S.add("pool", lambda e: e.dma_start(out=sgub_sb[0:1, :], in_=sgub), w=["sgub"], dma="c3")
    S.add("sp", lambda e: e.dma_start(out=sgug_sb, in_=sgug), w=["sgug"], dma="c5")

    xv = xall.rearrange("(c p) t -> p c t", p=128)
    for blk in range(8):
        S.add("pool", lambda e, blk=blk: e.dma_start(out=hnT[:, :, blk * 512:(blk + 1) * 512], in_=xv[:, :, blk * 512:(blk + 1) * 512]),
              w=[("hnT", blk), ("xq", blk % 2)], dma="xc%d" % blk)
    su = [load_w8(win[oc]) for oc in (0, 1)]
    sv = [load_w8(win[oc]) for oc in (2, 3)]
    S.add("pool", lambda e: e.dma_start(out=rope_sb, in_=rope), w=["rope", ("xq", 0), ("xq", 1)], dma="c1")
    a1_state = {}

    def a1_args(blk):
        return (lambda c, tb, blk=blk: hnT[:, c, blk * 512:(blk + 1) * 512], lambda c, tb, blk=blk: [("hnT", blk)])
    for blk in range(9):
        if blk < 8:
            sf, sk = a1_args(blk)
            a1_state[blk] = norm_s1(sf, sk, [(list(range(8)), 1024.0)], 0, sqAB[blk % 2], "sqc%d" % (blk % 2), rstd, blk % 2)
        if blk >= 1:
            sf, sk = a1_args(blk - 1)
            norm_s2(a1_state[blk - 1], sf, sk, 0, sf, sk, 0)
    S.barrier()

    RUN_A2 = stop not in ("A1",)
    RUN_A3 = stop not in ("A1", "A2")
    RUN_POST = stop not in ("A1", "A2", "A3")
    for h in range(4):
        S.add("dve", lambda e, h=h: e.tensor_tensor(wm[:, h * 128:(h + 1) * 128], sguw_sb[:, h * 128:(h + 1) * 128],
                                                    MASK[:, 0:128], ALU.mult), w=[("wm", h)])
    NTT = 16 if RUN_A2 else 0
    for tt in range(NTT):
        b = next_bank()

        def fnv(e, b=b, tt=tt):
            ins = None
            for vc in range(2):
                for c in range(8):
                    ins = e.matmul(bank(b)[:, vc * 128:(vc + 1) * 128], hnT[:, c, NT + tt * 128:NT + (tt + 1) * 128],
                                   w8[sv[vc]][:, c, :], start=(c == 0), stop=(c == 7))
            return ins
        S.add("pe", fnv, r=[("w8", sv[0]), ("w8", sv[1])], w=[("ps", b)])
        S.add("act", lambda e, b=b, tt=tt: e.activation(vgall[:, tt, :], bank(b)[:, 0:256], AF.Gelu_apprx_tanh),
              r=[("ps", b)], w=[("vg", tt)])
    for uc in range(2 if RUN_A2 else 0):
        for tb in range(4):
            b = next_bank()
            S.add("pe", mm_group(b, bank(b), [(w8[su[uc]][:, c, :], hnT[:, c, NT + tb * 512:NT + (tb + 1) * 512]) for c in range(8)]),
                  r=[("w8", su[uc])], w=[("ps", b)])
            S.add("act", lambda e, b=b, uc=uc, tb=tb: e.activation(uT[:, uc, tb * 512:(tb + 1) * 512], bank(b), AF.Gelu_apprx_tanh),
                  r=[("ps", b)], w=[("uT", uc, tb)])
    pre_w = {}
    if RUN_A2:
        S.add("pool", lambda e: e.memset(epsc, EPS), w=["epsc"])
        vgk = [("vg", tt) for tt in range(16)]
        vg_flat = vgall.rearrange("p t f -> p (t f)")
        sq_flat = sqall.rearrange("p t f -> p (t f)")
        S.add("act", lambda e: e.activation(sq_flat, vg_flat, AF.Square), r=vgk, w=["sqall"])
        S.add("dve", lambda e: e.tensor_reduce(sst[:, 0:16], vgall, AX.X, ALU.add), r=vgk, w=["s0"])
        S.add("dve", lambda e: e.tensor_reduce(sst[:, 32:48], sqall, AX.X, ALU.add), r=["sqall"], w=["s2"])
        S.add("dve", lambda e: e.tensor_scalar(sst[:, 16:32], sst[:, 0:16], 1.0 / 256.0, None, ALU.mult), r=["s0"], w=["s1"])
        S.add("dve", lambda e: e.tensor_tensor(sst[:, 48:64], sst[:, 16:32], sst[:, 16:32], ALU.mult), r=["s1"], w=["s3"])
        S.add("dve", lambda e: e.scalar_tensor_tensor(sst[:, 64:80], sst[:, 32:48], 1.0 / 256.0, sst[:, 48:64], ALU.mult, ALU.subtract),
              r=["s2", "s3"], w=["s4"])
        S.add("act", lambda e: e.activation(sst[:, 64:80], sst[:, 64:80], AF.Ln, bias=epsc[:, 0:1]), r=["s4", "epsc"], w=["s4"])
        S.add("act", lambda e: e.activation(sst[:, 80:96], sst[:, 64:80], AF.Exp, scale=-0.5), r=["s4"], w=["s5"])
        for tt in range(16):
            S.add("dve", lambda e, tt=tt: e.tensor_scalar(vgall[:, tt, :], vgall[:, tt, :], sst[:, 16 + tt:17 + tt], sst[:, 80 + tt:81 + tt],
                                                          ALU.subtract, ALU.mult),
                  r=[("vg", tt), "s1", "s5"], w=[("vg", tt)])
        S.add("dve", lambda e: e.tensor_tensor(vfall, vgall, sgug_sb.unsqueeze(1).to_broadcast([128, 16, 256]), ALU.mult),
              r=vgk, w=["vfall"])
        pre_w[0] = (load_w8(win[4]), load_w8(win[10]), load_w8(win[16]))
    for tt in range(NTT):
        b2 = next_bank()

        def fnm(e, b2=b2, tt=tt):
            ins = None
            e.matmul(bank(b2), ones_bf[0:1, :], sgub_sb[0:1, :], start=True, stop=False)
            for h in range(4):
                pc = h // 2
                o = bank(b2)[:, h * 128:(h + 1) * 128]
                ins = e.matmul(o, vfall[:, tt, pc * 128:(pc + 1) * 128], wm[:, h * 128:(h + 1) * 128], start=False, stop=(h == 3))
            return ins
        S.add("pe", fnm, r=["vfall"] + [("wm", h) for h in range(4)], w=[("ps", b2)])
        for hh in range(2):
            ps_ = slice(hh * 64, hh * 64 + 64)
            src_ = bank(b2)[ps_, :].rearrange("p (c x) -> p c x", x=256)[:, :, hh * 128:(hh + 1) * 128]
            S.add("dve", lambda e, ps_=ps_, src_=src_, tt=tt: e.tensor_tensor(
                yT[ps_, 0:2, tt * 128:(tt + 1) * 128], src_, uT[ps_, :, tt * 128:(tt + 1) * 128], ALU.mult),
                r=[("ps", b2), ("uT", 0, tt // 4), ("uT", 1, tt // 4)], w=[("yT", tt, hh)])
    S.barrier()

    jobs = attention_jobs()
    vt_slices = [None] * 69
    for (ks, vt, _q, _m) in jobs:
        vt_slices[vt] = ks
    S.add("pool", lambda e: e.memset(vaug[:, :, 64:128], 1.0), w=["vaug_ones"])
    S.add("pool", lambda e: e.memset(qA[64:128, :], 0.0), w=["qAz"])
    S.add("pool", lambda e: e.memset(qB[0:64, :], 0.0), w=["qBz"])
    mask_ap = {"cp": MASK[:, 0:256], "c": MASK[:, 0:128], "ph": MASK[:, 256:384]}
    PBANKS = [0, 1, 2, 3, 4, 5]
    pb_ctr = [0]

    def nextpb():
        b = PBANKS[pb_ctr[0] % len(PBANKS)]
        pb_ctr[0] += 1
        return b

    rr_ctr = [0]
    groups = []
    curg, curn = [], 0
    for jb in jobs:
        n = jb[2][1]
        if curn + n > 512:
            groups.append(curg)
            curg, curn = [], 0
        curg.append(jb)
        curn += n
    if curg:
        groups.append(curg)
    mbuf = {}
    msrc = {"cp": (0, 256), "c": (0, 128), "ph": (256, 128)}
    for grp in groups:
        sig = tuple(jb[3] for jb in grp)
        if sig in mbuf:
            continue
        mb_ = view(alloc(512 * 2), BF16, 512)
        mbuf[sig] = mb_
        o = 0
        for mk in sig:
            m0, mn = msrc[mk]
            S.add("dve", lambda e, mb_=mb_, o=o, m0=m0, mn=mn: e.tensor_copy(mb_[:, o:o + mn], MASK[:, m0:m0 + mn]), w=["mbuf"])
            o += mn
    deferred = []

    def drain(n):
        for _ in range(n):
            if deferred:
                a_ = deferred.pop(0)
                S.add(a_[0], a_[1], r=a_[2], w=a_[3])
    for c in range(npairs if RUN_A3 else 0):
        if c in pre_w:
            sq_, sk_, sv_ = pre_w[c]
        else:
            sq_ = load_w8(win[4 + c])
            sk_ = load_w8(win[10 + c])
            sv_ = load_w8(win[16 + c])
        blocks = [("q", sq_, tb, NT + tb * 512) for tb in range(4)] + [("k", sk_, tb, tb * 512) for tb in range(8)]
        pbank = {}

        def add_proj(i):
            which, slot, tb, t0_ = blocks[i]
            tok = slice(t0_, t0_ + 512)
            b = nextpb()
            pbank[i] = b
            S.add("pe", mm_group(b, bank(b), [(w8[slot][:, kc, :], hnT[:, kc, tok]) for kc in range(8)]),
                  r=[("w8", slot)], w=[("ps", b)])
        add_proj(0)
        for i in range(len(blocks)):
            if i + 1 < len(blocks):
                add_proj(i + 1)
            which, slot, tb, t0_ = blocks[i]
            tok = slice(t0_, t0_ + 512)
            b = pbank[i]
            rs_ = rr_ctr[0] % 2
            rr_ctr[0] += 1
            S.add("act", lambda e, b=b, rs_=rs_: e.activation(raw[rs_], bank(b), AF.Copy),
                  r=[("ps", b)], w=[("raw", rs_)])
            b2 = nextpb()
            S.add("pe", lambda e, b2=b2, rs_=rs_: e.matmul(bank(b2), rotm, raw[rs_], start=True, stop=True),
                  r=[("raw", rs_)], w=[("ps", b2)])
            S.add("dve", lambda e, b2=b2, rs_=rs_, tok=tok: e.tensor_tensor(t1[rs_], bank(b2), sinT[:, tok], ALU.mult),
                  r=[("ps", b2)], w=[("t1", rs_)])
            S.add("pool", lambda e, rs_=rs_, tok=tok: e.tensor_tensor(t2[rs_], raw[rs_], cosT[:, tok], ALU.mult),
                  r=[("raw", rs_)], w=[("t2", rs_)])
            if which == "q":
                qs = slice(tb * 512, (tb + 1) * 512)
                S.add("dve", lambda e, rs_=rs_, qs=qs: e.tensor_tensor(qA[0:64, qs], t1[rs_][0:64, :], t2[rs_][0:64, :], ALU.add),
                      r=[("t1", rs_), ("t2", rs_)], w=[("qA", tb)])
                S.add("dve", lambda e, rs_=rs_, qs=qs: e.tensor_tensor(qB[64:128, qs], t1[rs_][64:128, :], t2[rs_][64:128, :], ALU.add),
                      r=[("t1", rs_), ("t2", rs_)], w=[("qB", tb)])
            else:
                S.add("dve", lambda e, rs_=rs_, tok=tok: e.tensor_tensor(kT[:, tok], t1[rs_], t2[rs_], ALU.add),
                      r=[("t1", rs_), ("t2", rs_)], w=[("kT", tb)])
            drain(1)
        for tb in range(8):
            tok = slice(tb * 512, (tb + 1) * 512)
            b = nextpb()
            S.add("pe", mm_group(b, bank(b), [(w8[sv_][:, kc, :], hnT[:, kc, tok]) for kc in range(8)]),
                  r=[("w8", sv_)], w=[("ps", b)])
            S.add("act", lambda e, b=b, tok=tok: e.activation(vraw[:, tok], bank(b), AF.Copy),
                  r=[("ps", b)], w=[("vraw", tb)])
        vraw_keys = [("vraw", tb) for tb in range(8)]
        for g0 in range(0, 69, 8):
            g1 = min(69, g0 + 8)
            b = nextpb()

            def fnt(e, b=b, g0=g0, g1=g1):
                ins = None
                for j, vt in enumerate(range(g0, g1)):
                    ins = e.transpose(bankbf(b)[:, j * 128:(j + 1) * 128], vraw[:, vt_slices[vt]], ident)
                return ins
            S.add("pe", fnt, r=vraw_keys, w=[("ps", b)])
            n = g1 - g0
            src = bankbf(b)[:, 0:n * 128].rearrange("p (t h d) -> p t h d", h=2, d=64)
            dst = vaug[:, g0:g1, :].rearrange("p t (h d) -> p t h d", d=64)[:, :, 0::2, :]
            S.add("dve", lambda e, src=src, dst=dst: e.tensor_copy(dst, src),
                  r=[("ps", b)], w=[("vaug", g0 // 8)])
        for hd in range(2):
            qpad = qA if hd == 0 else qB
            qkey = "qA" if hd == 0 else "qB"
            osl = slice(0, 64) if hd == 0 else slice(64, 128)
            dsl = slice(64, 128) if hd == 0 else slice(0, 64)
            vcols = slice(0, 128) if hd == 0 else slice(64, 192)
            G = len(groups)
            ginfo = []
            for gi, grp in enumerate(groups):
                offs = []
                o = 0
                for jb in grp:
                    offs.append(o)
                    o += jb[2][1]
                ginfo.append((grp, offs, o, 4 + (gi % 4), gi % NPT))

            def add_S(gi):
                grp, offs, ntot, sb_, pti = ginfo[gi]

                def fns(e, grp=grp, offs=offs, sb_=sb_, qpad=qpad):
                    ins = None
                    for jb, of in zip(grp, offs):
                        ks, vt, (q0, n, st), mk = jb
                        ins = e.matmul(bank(sb_)[:, of:of + n], kT[:, ks], qpad[:, q0:q0 + (n - 1) * st + 1:st], start=True, stop=True)
                    return ins
                S.add("pe", fns, r=[("kT", t) for t in range(8)] + [(qkey, t) for t in range(4)] + ["qAz", "qBz"],
                      w=[("ps", sb_)])
                S.add("act", lambda e, sb_=sb_, pti=pti, ntot=ntot: e.activation(PT[pti][:, 0:ntot], bank(sb_)[:, 0:ntot], AF.Exp, scale=0.125),
                      r=[("ps", sb_)], w=[("PT", pti)])
                sig = tuple(jb[3] for jb in grp)
                S.add("dve", lambda e, pti=pti, ntot=ntot, sig=sig: e.tensor_tensor(
                    PT[pti][:, 0:ntot], PT[pti][:, 0:ntot], mbuf[sig][:, 0:ntot], ALU.mult),
                    r=[("PT", pti), "mbuf"], w=[("PT", pti)])
                drain(1)

            def add_PV(gi):
                grp, offs, ntot, sb_, pti = ginfo[gi]

                def fnpv(e, grp=grp, offs=offs, pti=pti, vcols=vcols):
                    ins = None
                    for jb, of in zip(grp, offs):
                        ks, vt, (q0, n, st), mk = jb
                        for (i0, cnt, col) in out_pieces(q0, n, st):
                            ins = e.matmul(psall[:, col:col + (cnt - 1) * st + 1:st], vaug[:, vt, vcols],
                                           PT[pti][:, of + i0:of + i0 + cnt], start=False, stop=False, skip_group_check=True)
                    return ins
                banks_ = sorted({col // 512 for jb in grp for (_i0, _cnt, col) in out_pieces(*jb[2])})
                for b_ in banks_:
                    if b_ not in zeroed:
                        zeroed.add(b_)
                        S.add("pe", lambda e, b_=b_: e.matmul(bank(b_), zeros_bf, kT[:, 0:512], start=True, stop=False, skip_group_check=True),
                              r=[("kT", 0)], w=[("ps", b_)])
                S.add("pe", fnpv, r=[("PT", pti), "vaug_ones"] + [("vaug", g) for g in range(9)],
                      w=[("ps", b_) for b_ in banks_])

            zeroed = set()
            LA = 4
            for gi in range(G + LA):
                if gi < G:
                    add_S(gi)
                if gi >= LA:
                    add_PV(gi - LA)
            assert zeroed == {0, 1, 2, 3}
            vk = [("vraw", t) for t in range(8)]
            for b_ in range(4):
                cs = slice(b_ * 512, (b_ + 1) * 512)
                if b_ % 2 == 0:
                    S.add("act", lambda e, b_=b_, cs=cs: e.activation(Tsb[:, cs], bank(b_), AF.Copy),
                          r=[("ps", b_)], w=vk[2 * b_:2 * b_ + 2])
                else:
                    S.add("dve", lambda e, b_=b_, cs=cs: e.tensor_copy(Tsb[:, cs], bank(b_)),
                          r=[("ps", b_)], w=vk[2 * b_:2 * b_ + 2])
            for b_ in range(4):
                rb = b_ % 2
                cs = slice(b_ * 512, (b_ + 1) * 512)
                deferred.append(("act", lambda e, dsl=dsl, cs=cs: e.activation(Tsb[dsl, cs], Tsb[dsl, cs], AF.Ln),
                                 vk[2 * b_:2 * b_ + 2], vk[2 * b_:2 * b_ + 2]))
                deferred.append(("act", lambda e, rb=rb, osl=osl, dsl=dsl, cs=cs: e.activation(rec[rb][osl, :], Tsb[dsl, cs], AF.Exp, scale=-1.0),
                                 vk[2 * b_:2 * b_ + 2], [("rec", rb)]))
                deferred.append(("dve", lambda e, rb=rb, osl=osl, cs=cs, c=c: e.tensor_tensor(yT[osl, 2 + c, cs], Tsb[osl, cs], rec[rb][osl, :], ALU.mult),
                                 vk[2 * b_:2 * b_ + 2] + [("rec", rb)], [("yTb", c, hd, b_)]))
    drain(len(deferred))
    S.barrier()

    S.add("pool", lambda e: e.dma_start(out=wpp_sb, in_=wpp), w=["wpp"], dma="wpp")
    w22_ctr = [0]
    tmp_ctr = [0]
    pT_v = pT.rearrange("(c p) t -> p c t", p=128)
    out_v = outT.rearrange("(c p) t -> p c t", p=128)
    G1 = [(list(range(8)), 1024.0)]

    def hsrc(c, tb):
        return hT[:, c, tb * 512:(tb + 1) * 512]

    def hkeys(c, tb):
        return [("hT", c, tb)]

    def nbsrc(c, tb):
        return hnb[:, c, tb * 512:(tb + 1) * 512]

    def nbkeys(c, tb):
        return [("hnb", c, tb)]

    def ld_hT(half, tb):
        c0_ = NT + half * HT + tb * 512
        S.add("sp", lambda e, c0_=c0_, tb=tb: e.dma_start(out=hT[:, :, tb * 512:(tb + 1) * 512], in_=xv[:, :, c0_:c0_ + 512]),
              w=[("hT", o, tb) for o in range(8)], dma="hT%d" % tb)

    def ld_pT(half):
        h0 = half * HT
        S.add("pool", lambda e, h0=h0: e.dma_start(out=pTh, in_=pT_v[:, :, h0:h0 + HT]), w=["pTh"], dma="pTh")

    YG = [([0, 1], 256.0), ([2, 3, 4, 5, 6, 7], 768.0)]

    def norm_steps(kind, half, tb):
        h0 = half * HT
        if kind == "yn":
            src = lambda c, tb_, h0=h0: yT[:, c, h0 + tb_ * 512:h0 + (tb_ + 1) * 512]
            skeys = lambda c, tb_: []
            gcol, groups = 8, YG
            dst = lambda c, tb_: aT[:, c, tb_ * 512:(tb_ + 1) * 512]
            dkeys = lambda c, tb_: [("aT", c, tb_)]
        elif kind == "E":
            src, skeys, gcol, dst, dkeys, groups = hsrc, hkeys, 32, hsrc, hkeys, G1
        else:
            src, skeys, gcol, dst, dkeys, groups = hsrc, hkeys, (16 if kind == "nC" else 24), nbsrc, nbkeys, G1
        box = {}
        slot = (tb * len(groups)) % len(rstd2)

        def s1():
            box["st"] = norm_s1(src, skeys, groups, tb, sq2, "sqc2", rstd2, slot)

        def s2():
            norm_s2(box["st"], src, skeys, gcol, dst, dkeys, tb)
            if kind == "E":
                S.add("sp", lambda e, h0=h0, tb=tb: e.dma_start(out=out_v[:, :, h0 + tb * 512:h0 + (tb + 1) * 512],
                                                                 in_=hT[:, :, tb * 512:(tb + 1) * 512]),
                      r=[("hT", o, tb) for o in range(8)], dma="out")
        return s1, s2

    def steps_B(tb):
        ts_ = slice(tb * 512, (tb + 1) * 512)

        def mk(o):
            def st():
                s_ = load_w8(wout[o])
                b = next_bank()
                S.add("pe", mm_group(b, bank(b), [(w8[s_][:, kc, :], aT[:, kc, ts_]) for kc in range(8)]),
                      r=[("w8", s_)] + [("aT", kc, tb) for kc in range(8)], w=[("ps", b)])
                S.add("dve", lambda e, b=b: e.tensor_tensor(hT[:, o, ts_], hT[:, o, ts_], bank(b), ALU.add),
                      r=[("ps", b), ("hT", o, tb)], w=[("hT", o, tb)])
            return st
        return [mk(o) for o in range(8)]

    def steps_GU(tb):
        ts_ = slice(tb * 512, (tb + 1) * 512)

        def mk(f):
            def st():
                sg = load_w8(wgate[f])
                su_ = load_w8(wup[f])
                bg = next_bank()
                S.add("pe", mm_group(bg, bank(bg), [(w8[sg][:, kc, :], hnb[:, kc, ts_]) for kc in range(8)]),
                      r=[("w8", sg)] + [("hnb", kc, tb) for kc in range(8)], w=[("ps", bg)])
                bu = next_bank()
                S.add("pe", mm_group(bu, bank(bu), [(w8[su_][:, kc, :], hnb[:, kc, ts_]) for kc in range(8)]),
                      r=[("w8", su_)] + [("hnb", kc, tb) for kc in range(8)], w=[("ps", bu)])
                tsl = tmp_ctr[0] % 2
                tmp_ctr[0] += 1
                S.add("act", lambda e, bg=bg, tsl=tsl: e.activation(tmpf[tsl], bank(bg), AF.Silu),
                      r=[("ps", bg)], w=[("tmpf", tsl)])
                S.add("dve", lambda e, bu=bu, tsl=tsl: e.tensor_tensor(aT[:, f, ts_], tmpf[tsl], bank(bu), ALU.mult),
                      r=[("ps", bu), ("tmpf", tsl)], w=[("aT", f, tb)])
            return st
        return [mk(f) for f in range(22)]

    def steps_DN(tb):
        ts_ = slice(tb * 512, (tb + 1) * 512)

        def mk(o):
            def st():
                b = next_bank()
                for hf in range(2):
                    s_ = w22_ctr[0] % 4
                    w22_ctr[0] += 1
                    S.add("pool", lambda e, s_=s_, hf=hf: e.dma_start(out=w22h[s_], in_=wdown[o][:, hf * 11:(hf + 1) * 11, :]),
                          w=[("w22h", s_)], dma="w22h_%d" % s_)

                    def fn(e, s_=s_, hf=hf, b=b):
                        ins = None
                        for j in range(11):
                            f = hf * 11 + j
                            ins = e.matmul(bank(b), w22h[s_][:, j, :], aT[:, f, ts_], start=(f == 0), stop=(f == 21))
                        return ins
                    S.add("pe", fn, r=[("w22h", s_)] + [("aT", f, tb) for f in range(hf * 11, hf * 11 + 11)], w=[("ps", b)])
                S.add("dve", lambda e, b=b: e.tensor_tensor(hT[:, o, ts_], hT[:, o, ts_], bank(b), ALU.add),
                      r=[("ps", b), ("hT", o, tb)], w=[("hT", o, tb)])
            return st
        return [mk(o) for o in range(8)]

    def steps_D(tb):
        ts_ = slice(tb * 512, (tb + 1) * 512)

        def mk(o):
            def st():
                s_ = load_w8(wpg[o])
                bg = next_bank()
                S.add("pe", mm_group(bg, bank(bg), [(w8[s_][:, kc, :], hnb[:, kc, ts_]) for kc in range(8)]),
                      r=[("w8", s_)] + [("hnb", kc, tb) for kc in range(8)], w=[("ps", bg)])
                bp = next_bank()
                S.add("pe", mm_group(bp, bank(bp), [(wpp_sb[:, kc, o * 128:(o + 1) * 128], pTh[:, kc, ts_]) for kc in range(2)]),
                      r=["wpp", "pTh"], w=[("ps", bp)])
                tsl = tmp_ctr[0] % 2
                tmp_ctr[0] += 1
                S.add("act", lambda e, bg=bg, tsl=tsl: e.activation(tmpf[tsl], bank(bg), AF.Sigmoid),
                      r=[("ps", bg)], w=[("tmpf", tsl)])
                S.add("dve", lambda e, bp=bp, tsl=tsl: e.tensor_tensor(tmpf[tsl], tmpf[tsl], bank(bp), ALU.mult),
                      r=[("ps", bp), ("tmpf", tsl)], w=[("tmpf", tsl)])
                S.add("dve", lambda e, tsl=tsl: e.tensor_tensor(hT[:, o, ts_], hT[:, o, ts_], tmpf[tsl], ALU.add),
                      r=[("tmpf", tsl), ("hT", o, tb)], w=[("hT", o, tb)])
            return st
        return [mk(o) for o in range(8)]

    def run(steps, inserts=None):
        inserts = inserts or {}
        for i, st in enumerate(steps):
            st()
            for x in inserts.get(i, ()):
                x()

    if RUN_POST:
        N = norm_steps
        ld_hT(0, 0); ld_hT(0, 1); ld_pT(0)
        for tb in range(2):
            a, b_ = N("yn", 0, tb)
            a(); b_()
        for half in range(2):
            nC0 = N("nC", half, 0); nC1 = N("nC", half, 1)
            nD0 = N("nD", half, 0); nD1 = N("nD", half, 1)
            E0 = N("E", half, 0); E1 = N("E", half, 1)
            if half == 0:
                run(steps_B(0))
            run(steps_B(1), {2: [nC0[0]], 5: [nC0[1]]})
            run(steps_GU(0), {2: [nC1[0]], 5: [nC1[1]]})
            run(steps_GU(1))
            run(steps_DN(0))
            run(steps_DN(1), {2: [nD0[0]], 5: [nD0[1]]})
            if half == 0:
                yn0 = N("yn", 1, 0); yn1 = N("yn", 1, 1)
                run(steps_D(0), {1: [nD1[0]], 3: [nD1[1]], 5: [yn0[0]], 7: [yn0[1]]})
                run(steps_D(1), {1: [E0[0]], 3: [E0[1], lambda: ld_hT(1, 0)], 5: [yn1[0]], 7: [yn1[1]]})
                run(steps_B(0), {1: [E1[0]], 3: [E1[1], lambda: ld_hT(1, 1), lambda: ld_pT(1)]})
            else:
                run(steps_D(0), {2: [nD1[0]], 5: [nD1[1]]})
                run(steps_D(1), {2: [E0[0]], 5: [E0[1]]})
                E1[0](); E1[1]()
    S.barrier()
    if dbg is not None:
        dbuf = debug[0](locals())
        S.add("pool", lambda e: e.dma_start(out=dbg, in_=dbuf), dma="dbg")
        S.barrier()
    S.finalize()

    sems = {}
    for e in ENGS:
        sems[("eng", e)] = es.enter_context(nc.semaphore("s_" + e))
    for k in S.dma_cnt:
        sems[("dma", k)] = es.enter_context(nc.semaphore("d_" + k))
    with es:
        with nc.Block() as block:
            @block.tensor
            def _(t):
                S.emit("pe", t, sems)

            @block.scalar
            def _(s):
                S.emit("act", s, sems)

            @block.vector
            def _(v):
                S.emit("dve", v, sems)

            @block.gpsimd
            def _(g):
                S.emit("pool", g, sems)

            @block.sync
            def _(sy):
                S.emit("sp", sy, sems)
                for k, cnt in S.dma_cnt.items():
                    sy.wait_ge(sems[("dma", k)], 16 * cnt)
    return nc


def _tile_w(w, kc):
    K, N = w.shape
    return np.ascontiguousarray(w.reshape(kc, 128, N // 128, 128).transpose(2, 1, 0, 3))


def prep_inputs(x, p, mix_norm_g, w_in, sgu_w, sgu_b, sgu_norm_g, out_norm_a, out_norm_b,
                w_out, ffn_norm_g, w_gate, w_up, w_down, ple_norm_g, w_ple_gate,
                w_ple_proj, final_norm_g):
    f32 = np.float32
    x = np.asarray(x, f32); p = np.asarray(p, f32)
    shared = {
        "win": _tile_w(np.asarray(w_in[0], f32), 8),
        "wout": _tile_w(np.asarray(w_out[0], f32), 8),
        "wgate": _tile_w(np.asarray(w_gate[0], f32), 8),
        "wup": _tile_w(np.asarray(w_up[0], f32), 8),
        "wdown": _tile_w(np.asarray(w_down[0], f32), 22),
        "wpg": _tile_w(np.asarray(w_ple_gate[0], f32), 8),
        "wpp": np.ascontiguousarray(np.asarray(w_ple_proj[0], f32).reshape(2, 128, 1024).transpose(1, 0, 2)),
    }
    gcols = np.concatenate([
        np.asarray(mix_norm_g[0], f32).reshape(8, 128),
        np.concatenate([np.asarray(out_norm_a[0], f32), np.asarray(out_norm_b[0], f32)]).reshape(8, 128),
        np.asarray(ffn_norm_g[0], f32).reshape(8, 128),
        np.asarray(ple_norm_g[0], f32).reshape(8, 128),
        np.asarray(final_norm_g, f32).reshape(8, 128)], axis=0)
    shared["gains"] = np.ascontiguousarray(gcols.T)
    shared["sgug"] = np.ascontiguousarray(np.broadcast_to(np.asarray(sgu_norm_g[0], f32)[None, :], (128, 256)))
    shared["sguw"] = np.ascontiguousarray(np.asarray(sgu_w[0], f32).transpose(2, 0, 1).reshape(128, 512))
    shared["sgub"] = np.ascontiguousarray(np.asarray(sgu_b[0], f32).reshape(1, 512))
    ii = np.arange(128)
    ident = np.eye(128, dtype=f32)
    rotm = np.zeros((128, 128), f32)
    for m in range(128):
        if (m % 64) < 32:
            rotm[m + 32, m] = -1.0
        else:
            rotm[m - 32, m] = 1.0
    mcur = (ii[None, :] >= ii[:, None]).astype(f32)
    mprev = (ii[None, :] <= ii[:, None]).astype(f32)
    inv = (10000.0 ** (-np.arange(32, dtype=f32) / f32(32))).astype(f32)
    in_maps = []
    for core in range(NCORES):
        b, nci = core // 4, core % 4
        own = x[b, nci * NT:(nci + 1) * NT, :]
        if nci > 0:
            halo = x[b, (nci - 1) * NT:nci * NT, :]
        else:
            halo = np.zeros_like(own)
        xall = np.ascontiguousarray(np.concatenate([halo, own], axis=0).T)
        pTc = np.ascontiguousarray(p[0, b, nci * NT:(nci + 1) * NT, :].T)
        mph = mprev if nci > 0 else np.zeros_like(mprev)
        cstc = np.concatenate([ident, rotm, mcur, mprev * 0 + mprev, mph, np.ones((128, 128), f32), np.zeros((128, 128), f32)], axis=1)
        cstc[:, 384:512] = mprev
        pos = (np.arange(NA, dtype=np.int64) + (nci - 1) * NT).astype(f32)
        ang = pos[None, :] * inv[:, None]
        cs = np.cos(ang).astype(f32); sn = np.sin(ang).astype(f32)
        ropec = np.stack([np.tile(cs, (4, 1)), np.tile(sn, (4, 1))], axis=1)
        m = dict(shared)
        m["xall"] = xall
        m["pT"] = pTc
        m["cst"] = np.ascontiguousarray(cstc)
        m["rope"] = np.ascontiguousarray(ropec.astype(f32))
        in_maps.append(m)
    return in_maps


_NC_CACHE = {}


def kernel(**inputs):
    in_maps = prep_inputs(**inputs)
    if "nc" not in _NC_CACHE:
        _NC_CACHE["nc"] = build_nc()
    nc = _NC_CACHE["nc"]
    res = run_bass_kernel_spmd(nc, in_maps, core_ids=list(range(NCORES)))
    out = np.empty((2, 8192, D), np.float32)
    for core in range(NCORES):
        b, nci = core // 4, core % 4
        out[b, nci * NT:(nci + 1) * NT, :] = np.asarray(res.results[core]["outT"]).T
    return out
```

```python
import numpy as np
import concourse.bass as bass
import concourse.mybir as mybir
from concourse.bass_utils import run_bass_kernel_spmd

F32 = mybir.dt.float32
BF16 = mybir.dt.bfloat16
U8 = mybir.dt.uint8
AF = mybir.ActivationFunctionType
ALU = mybir.AluOpType
AX = mybir.AxisListType

D = 1024
NT = 2048
NA = 4096
DFF = 2816
EPS = 1e-6
NCORES = 8

ENGS = ["pe", "act", "dve", "pool", "sp"]


class Op:
    __slots__ = ("eng", "fn", "deps", "sig", "sigval", "dma", "dma_val", "idx")

    def __init__(self, eng, fn, dma):
        self.eng = eng
        self.fn = fn
        self.dma = dma
        self.deps = []
        self.sig = False
        self.sigval = 0
        self.dma_val = 0
        self.idx = 0


class Sched:
    def __init__(self):
        self.ops = {e: [] for e in ENGS}
        self.lastw = {}
        self.readers = {}
        self.dma_cnt = {}
        self.dma_last = {}
        self.pending = {e: [] for e in ENGS}

    def add(self, eng, fn, r=(), w=(), dma=None):
        op = Op(eng, fn, dma)
        deps = set(self.pending[eng])
        self.pending[eng] = []
        for k in r:
            d = self.lastw.get(k)
            if d is not None:
                deps.add(d)
        for k in w:
            d = self.lastw.get(k)
            if d is not None:
                deps.add(d)
            for rd in self.readers.get(k, ()):
                deps.add(rd)
        best = {}
        for d in deps:
            if d is op:
                continue
            if d.dma is None:
                if d.eng == "pe" and eng == "pe" and dma is None:
                    continue
                key = ("eng", d.eng)
                if key not in best or best[key].idx < d.idx:
                    best[key] = d
            else:
                key = ("dma", d.dma)
                if key not in best or best[key].dma_val < d.dma_val:
                    best[key] = d
        op.deps = list(best.values())
        for k in w:
            self.lastw[k] = op
            self.readers[k] = []
        for k in r:
            if k not in w:
                self.readers.setdefault(k, []).append(op)
        if dma is not None:
            c = self.dma_cnt.get(dma, 0) + 1
            self.dma_cnt[dma] = c
            op.dma_val = 16 * c
            self.dma_last[dma] = op
        op.idx = len(self.ops[eng])
        self.ops[eng].append(op)
        return op

    def barrier(self):
        lst = []
        for e in ENGS:
            for op in reversed(self.ops[e]):
                if op.dma is None:
                    lst.append(op)
                    break
        lst.extend(self.dma_last.values())
        for e in ENGS:
            self.pending[e] = list(lst)
        self.lastw = {}
        self.readers = {}

    def finalize(self):
        for e in ENGS:
            for op in self.ops[e]:
                for d in op.deps:
                    if d.dma is None:
                        d.sig = True
        for e in ENGS:
            c = 0
            for op in self.ops[e]:
                if op.sig:
                    c += 1
                    op.sigval = c

    def emit(self, ename, eng, sems):
        waited = {}
        for op in self.ops[ename]:
            for d in op.deps:
                if d.dma is not None:
                    key = ("dma", d.dma)
                    val = d.dma_val
                else:
                    key = ("eng", d.eng)
                    val = d.sigval
                if waited.get(key, 0) < val:
                    eng.wait_ge(sems[key], val)
                    waited[key] = val
            ins = op.fn(eng)
            if op.dma is not None:
                ins.then_inc(sems[("dma", op.dma)], 16)
            elif op.sig:
                ins.then_inc(sems[("eng", ename)], 1)


def attention_jobs():
    jobs = []
    vt = 0
    for kb in range(-1, 16):
        ks = slice(NT + kb * 128, NT + kb * 128 + 128, 1)
        if kb == -1:
            jobs.append((ks, vt, (0, 128, 1), "ph"))
        elif kb == 15:
            jobs.append((ks, vt, (15 * 128, 128, 1), "c"))
        else:
            jobs.append((ks, vt, (kb * 128, 256, 1), "cp"))
        vt += 1
    for r in range(4):
        for n in range(-1, 4):
            k0 = NT + n * 512 + r
            ks = slice(k0, k0 + 127 * 4 + 1, 4)
            if n == -1:
                jobs.append((ks, vt, (r, 128, 4), "ph"))
            elif n == 3:
                jobs.append((ks, vt, (1536 + r, 128, 4), "c"))
            else:
                jobs.append((ks, vt, (n * 512 + r, 256, 4), "cp"))
            vt += 1
    for r in range(16):
        for n in (-1, 0):
            k0 = (n + 1) * NT + r
            ks = slice(k0, k0 + 127 * 16 + 1, 16)
            jobs.append((ks, vt, (r, 128, 16), "ph" if n == -1 else "c"))
            vt += 1
    assert vt == 69
    return jobs


def out_pieces(q0, n, st):
    pieces = []
    i = 0
    while i < n:
        col = q0 + i * st
        bank_end = (col // 512 + 1) * 512
        cnt = min(n - i, (bank_end - col + st - 1) // st)
        pieces.append((i, cnt, col))
        i += cnt
    return pieces


def build_nc(debug=None, stop=None, npairs=6, nhalves=2):
    nc = bass.Bass("TRN2", target_bir_lowering=False)
    dr = {}

    def din(name, shape):
        dr[name] = nc.dram_tensor(name, list(shape), F32, kind="ExternalInput").ap()
        return dr[name]

    xall = din("xall", [D, NA])
    pT = din("pT", [256, NT])
    win = din("win", [22, 128, 8, 128])
    wout = din("wout", [8, 128, 8, 128])
    wgate = din("wgate", [22, 128, 8, 128])
    wup = din("wup", [22, 128, 8, 128])
    wdown = din("wdown", [8, 128, 22, 128])
    wpg = din("wpg", [8, 128, 8, 128])
    wpp = din("wpp", [128, 2, 1024])
    gains = din("gains", [128, 40])
    sgug = din("sgug", [128, 256])
    sguw = din("sguw", [128, 512])
    sgub = din("sgub", [1, 512])
    cst = din("cst", [128, 896])
    rope = din("rope", [128, 2, NA])
    outT = nc.dram_tensor("outT", [D, NT], F32, kind="ExternalOutput").ap()
    dbg = None
    if debug is not None:
        dbg = nc.dram_tensor("dbg", [128, debug[1]], F32, kind="ExternalOutput").ap()

    S = Sched()
    from contextlib import ExitStack
    es = ExitStack()
    TOTAL = 206 * 1024
    arena = es.enter_context(nc.sbuf_tensor("arena", [128, TOTAL], U8))
    psall = es.enter_context(nc.psum_tensor("psall", [128, 4096], F32))

    def bank(i):
        return psall[:, i * 512:(i + 1) * 512]

    def bankbf(i):
        return psall[:, i * 512:(i + 1) * 512].bitcast(BF16)

    cur = [0]

    def alloc(nbytes, at=None):
        if at is not None:
            return at
        off = cur[0]
        cur[0] = off + ((nbytes + 63) // 64) * 64
        assert cur[0] <= TOTAL, (cur[0], TOTAL)
        return off

    def view(off, dt, n, inner=None):
        sz = 2 if dt == BF16 else 4
        v = arena[:, off:off + n * sz].bitcast(dt)
        if inner is not None:
            v = v.rearrange("p (a b) -> p a b", b=inner)
        return v

    o_cstb = alloc(896 * 2); cstb = view(o_cstb, BF16, 896)
    ident = cstb[:, 0:128]; rotm = cstb[:, 128:256]
    MASK = cstb[:, 256:640]
    ones_bf = cstb[:, 640:768]; zeros_bf = cstb[:, 768:896]
    o_gn = alloc(40 * 4); gn = view(o_gn, F32, 40)
    o_sgug = alloc(256 * 4); sgug_sb = view(o_sgug, F32, 256)
    o_sguw = alloc(512 * 2); sguw_sb = view(o_sguw, BF16, 512)
    o_wm = alloc(512 * 2); wm = view(o_wm, BF16, 512)
    o_sgub = alloc(512 * 2); sgub_sb = view(o_sgub, BF16, 512)
    o_rope = alloc(2 * NA * 2); rope_sb = view(o_rope, BF16, 2 * NA, NA)
    cosT = rope_sb[:, 0, :]; sinT = rope_sb[:, 1, :]
    o_yT = alloc(8 * NT * 2); yT = view(o_yT, BF16, 8 * NT, NT)
    NW8 = 6
    o_w8 = [alloc(8 * 128 * 2) for _ in range(NW8)]
    w8 = [view(o, BF16, 1024, 128) for o in o_w8]
    NPT = 5
    o_pt = [alloc(512 * 2) for _ in range(NPT)]
    PT = [view(o, BF16, 512) for o in o_pt]
    big0 = cur[0]
    o_hn = alloc(8 * NA * 2); hnT = view(o_hn, BF16, 8 * NA, NA)
    X0 = cur[0]
    cur[0] = X0
    o_sqAB = [alloc(8 * 512 * 2) for _ in range(2)]
    sqAB = [view(o, BF16, 4096, 512) for o in o_sqAB]
    o_rs = [alloc(512 * 4) for _ in range(2)]
    rstd = [view(o, F32, 512) for o in o_rs]
    endA1 = cur[0]
    cur[0] = X0
    o_uT = alloc(2 * NT * 2); uT = view(o_uT, BF16, 2 * NT, NT)
    o_vgall = alloc(16 * 256 * 4); vgall = view(o_vgall, F32, 4096, 256)
    o_sqall = alloc(16 * 256 * 4); sqall = view(o_sqall, F32, 4096, 256)
    o_vfall = alloc(16 * 256 * 2); vfall = view(o_vfall, BF16, 4096, 256)
    o_sst = alloc(96 * 4); sst = view(o_sst, F32, 96)
    o_eps = alloc(64); epsc = view(o_eps, F32, 16)
    endSGU = cur[0]
    cur[0] = X0
    o_qA = alloc(NT * 2); qA = view(o_qA, BF16, NT)
    o_qB = alloc(NT * 2); qB = view(o_qB, BF16, NT)
    o_kT = alloc(NA * 2); kT = view(o_kT, BF16, NA)
    o_vr = alloc(NA * 2); vraw = view(o_vr, BF16, NA)
    Tsb = view(o_vr, F32, NT)
    o_va = alloc(69 * 192 * 2); vaug = view(o_va, BF16, 69 * 192, 192)
    o_raw = [alloc(512 * 2) for _ in range(2)]; raw = [view(o, BF16, 512) for o in o_raw]
    o_t1 = [alloc(512 * 4) for _ in range(2)]; t1 = [view(o, F32, 512) for o in o_t1]
    o_t2 = [alloc(512 * 4) for _ in range(2)]; t2 = [view(o, F32, 512) for o in o_t2]
    o_rec = [alloc(512 * 4) for _ in range(2)]; rec = [view(o, F32, 512) for o in o_rec]
    endATT = cur[0]
    cur[0] = big0
    HT = 1024
    o_hT = alloc(8 * HT * 4); hT = view(o_hT, F32, 8 * HT, HT)
    o_hnb = alloc(8 * HT * 2); hnb = view(o_hnb, BF16, 8 * HT, HT)
    o_aT = alloc(22 * HT * 2); aT = view(o_aT, BF16, 22 * HT, HT)
    sq2 = view(o_rope, BF16, 4096, 512)
    o_rs2 = [alloc(512 * 4) for _ in range(4)]; rstd2 = [view(o, F32, 512) for o in o_rs2]
    o_w22h = [alloc(11 * 128 * 2) for _ in range(4)]; w22h = [view(o, BF16, 11 * 128, 128) for o in o_w22h]
    o_wpp = alloc(2 * 1024 * 2); wpp_sb = view(o_wpp, BF16, 2048, 1024)
    o_pth = alloc(2 * HT * 2); pTh = view(o_pth, BF16, 2 * HT, HT)
    o_tmp = [alloc(512 * 4) for _ in range(2)]; tmpf = [view(o, F32, 512) for o in o_tmp]
    endPOST = cur[0]
    assert max(endA1, endSGU, endATT, endPOST) <= TOTAL, (endA1, endSGU, endATT, endPOST, TOTAL)
    cur[0] = max(endA1, endSGU, endATT, endPOST)

    w8_ctr = [0]

    def load_w8(src_ap):
        s = w8_ctr[0] % NW8
        w8_ctr[0] += 1
        S.add("pool", lambda e, s=s, src_ap=src_ap: e.dma_start(out=w8[s], in_=src_ap),
              w=[("w8", s)], dma="w8_%d" % s)
        return s

    bank_ctr = [0]
    gen_banks = [0, 1, 2, 3, 4, 5, 6, 7]

    def next_bank():
        b = gen_banks[bank_ctr[0] % len(gen_banks)]
        bank_ctr[0] += 1
        return b

    def mm_group(b, outap, pairs, extra_r=()):
        n = len(pairs)

        def fn(e):
            ins = None
            for i, (l, r_) in enumerate(pairs):
                ins = e.matmul(outap, l, r_, start=(i == 0), stop=(i == n - 1))
            return ins
        return fn

    def norm_s1(src_fn, src_keys, groups, tb, sqb, sqk, rsb, slot):
        st = []
        for gi, (chunks, div) in enumerate(groups):
            for c in chunks:
                S.add("act", lambda e, c=c, tb=tb: e.activation(sqb[:, c, :], src_fn(c, tb), AF.Square),
                      r=list(src_keys(c, tb)), w=[(sqk, c)])
            b = next_bank()
            S.add("pe", mm_group(b, bank(b), [(ones_bf, sqb[:, c, :]) for c in chunks]),
                  r=[(sqk, c) for c in chunks] + ["cst"], w=[("ps", b)])
            si = (slot + gi) % len(rsb)
            rr = rsb[si]
            rk = ("rs", id(rsb), si)
            S.add("dve", lambda e, b=b, rr=rr, div=div: e.tensor_scalar(rr, bank(b), 1.0 / div, EPS, ALU.mult, ALU.add),
                  r=[("ps", b)], w=[rk])
            st.append((chunks, rr, rk))
        return st

    def norm_s2(st, src_fn, src_keys, gcol, dst_fn, dst_keys, tb):
        for (chunks, rr, rk) in st:
            S.add("act", lambda e, rr=rr: e.activation(rr, rr, AF.Ln), r=[rk], w=[rk])
            S.add("act", lambda e, rr=rr: e.activation(rr, rr, AF.Exp, scale=-0.5), r=[rk], w=[rk])
            for c in chunks:
                S.add("dve", lambda e, c=c, tb=tb, rr=rr: e.scalar_tensor_tensor(
                    dst_fn(c, tb), src_fn(c, tb), gn[:, gcol + c:gcol + c + 1], rr, ALU.mult, ALU.mult),
                    r=list(src_keys(c, tb)) + [rk, "gn"], w=list(dst_keys(c, tb)))

    def rms_norm(src_fn, src_keys, gcol, dst_fn, dst_keys, groups, tbs, sqb, sqk, rsb):
        for tb in tbs:
            st = norm_s1(src_fn, src_keys, groups, tb, sqb, sqk, rsb, (tb * len(groups)) % len(rsb))
            norm_s2(st, src_fn, src_keys, gcol, dst_fn, dst_keys, tb)

    S.add("pool", lambda e: e.dma_start(out=cstb, in_=cst), w=["cst"], dma="c0")
    S.add("sp", lambda e: e.dma_start(out=gn, in_=gains), w=["gn"], dma="c4")
    S.add("pool", lambda e: e.dma_start(out=sguw_sb, in_=sguw), w=["sguw"], dma="c2")
    S.add("pool", lambda e: e.dma_start(out=sgub_sb[0:1, :], in_=sgub), w=["sgub"], dma="c3")
    S.add("sp", lambda e: e.dma_start(out=sgug_sb, in_=sgug), w=["sgug"], dma="c5")

    xv = xall.rearrange("(c p) t -> p c t", p=128)
    for blk in range(8):
        S.add("pool", lambda e, blk=blk: e.dma_start(out=hnT[:, :, blk * 512:(blk + 1) * 512], in_=xv[:, :, blk * 512:(blk + 1) * 512]),
              w=[("hnT", blk), ("xq", blk % 2)], dma="xc%d" % blk)
    su = [load_w8(win[oc]) for oc in (0, 1)]
    sv = [load_w8(win[oc]) for oc in (2, 3)]
    S.add("pool", lambda e: e.dma_start(out=rope_sb, in_=rope), w=["rope", ("xq", 0), ("xq", 1)], dma="c1")
    a1_state = {}

    def a1_args(blk):
        return (lambda c, tb, blk=blk: hnT[:, c, blk * 512:(blk + 1) * 512], lambda c, tb, blk=blk: [("hnT", blk)])
    for blk in range(9):
        if blk < 8:
            sf, sk = a1_args(blk)
            a1_state[blk] = norm_s1(sf, sk, [(list(range(8)), 1024.0)], 0, sqAB[blk % 2], "sqc%d" % (blk % 2), rstd, blk % 2)
        if blk >= 1:
            sf, sk = a1_args(blk - 1)
            norm_s2(a1_state[blk - 1], sf, sk, 0, sf, sk, 0)
    S.barrier()

    RUN_A2 = stop not in ("A1",)
    RUN_A3 = stop not in ("A1", "A2")
    RUN_POST = stop not in ("A1", "A2", "A3")
    for h in range(4):
        S.add("dve", lambda e, h=h: e.tensor_tensor(wm[:, h * 128:(h + 1) * 128], sguw_sb[:, h * 128:(h + 1) * 128],
                                                    MASK[:, 0:128], ALU.mult), w=[("wm", h)])
    NTT = 16 if RUN_A2 else 0
    for tt in range(NTT):
        b = next_bank()

        def fnv(e, b=b, tt=tt):
            ins = None
            for vc in range(2):
                for c in range(8):
                    ins = e.matmul(bank(b)[:, vc * 128:(vc + 1) * 128], hnT[:, c, NT + tt * 128:NT + (tt + 1) * 128],
                                   w8[sv[vc]][:, c, :], start=(c == 0), stop=(c == 7))
            return ins
        S.add("pe", fnv, r=[("w8", sv[0]), ("w8", sv[1])], w=[("ps", b)])
        S.add("act", lambda e, b=b, tt=tt: e.activation(vgall[:, tt, :], bank(b)[:, 0:256], AF.Gelu_apprx_tanh),
              r=[("ps", b)], w=[("vg", tt)])
    for uc in range(2 if RUN_A2 else 0):
        for tb in range(4):
            b = next_bank()
            S.add("pe", mm_group(b, bank(b), [(w8[su[uc]][:, c, :], hnT[:, c, NT + tb * 512:NT + (tb + 1) * 512]) for c in range(8)]),
                  r=[("w8", su[uc])], w=[("ps", b)])
            S.add("act", lambda e, b=b, uc=uc, tb=tb: e.activation(uT[:, uc, tb * 512:(tb + 1) * 512], bank(b), AF.Gelu_apprx_tanh),
                  r=[("ps", b)], w=[("uT", uc, tb)])
    pre_w = {}
    if RUN_A2:
        S.add("pool", lambda e: e.memset(epsc, EPS), w=["epsc"])
        vgk = [("vg", tt) for tt in range(16)]
        vg_flat = vgall.rearrange("p t f -> p (t f)")
        sq_flat = sqall.rearrange("p t f -> p (t f)")
        S.add("act", lambda e: e.activation(sq_flat, vg_flat, AF.Square), r=vgk, w=["sqall"])
        S.add("dve", lambda e: e.tensor_reduce(sst[:, 0:16], vgall, AX.X, ALU.add), r=vgk, w=["s0"])
        S.add("dve", lambda e: e.tensor_reduce(sst[:, 32:48], sqall, AX.X, ALU.add), r=["sqall"], w=["s2"])
        S.add("dve", lambda e: e.tensor_scalar(sst[:, 16:32], sst[:, 0:16], 1.0 / 256.0, None, ALU.mult), r=["s0"], w=["s1"])
        S.add("dve", lambda e: e.tensor_tensor(sst[:, 48:64], sst[:, 16:32], sst[:, 16:32], ALU.mult), r=["s1"], w=["s3"])
        S.add("dve", lambda e: e.scalar_tensor_tensor(sst[:, 64:80], sst[:, 32:48], 1.0 / 256.0, sst[:, 48:64], ALU.mult, ALU.subtract),
              r=["s2", "s3"], w=["s4"])
        S.add("act", lambda e: e.activation(sst[:, 64:80], sst[:, 64:80], AF.Ln, bias=epsc[:, 0:1]), r=["s4", "epsc"], w=["s4"])
        S.add("act", lambda e: e.activation(sst[:, 80:96], sst[:, 64:80], AF.Exp, scale=-0.5), r=["s4"], w=["s5"])
        for tt in range(16):
            S.add("dve", lambda e, tt=tt: e.tensor_scalar(vgall[:, tt, :], vgall[:, tt, :], sst[:, 16 + tt:17 + tt], sst[:, 80 + tt:81 + tt],
                                                          ALU.subtract, ALU.mult),
                  r=[("vg", tt), "s1", "s5"], w=[("vg", tt)])
        S.add("dve", lambda e: e.tensor_tensor(vfall, vgall, sgug_sb.unsqueeze(1).to_broadcast([128, 16, 256]), ALU.mult),
              r=vgk, w=["vfall"])
        pre_w[0] = (load_w8(win[4]), load_w8(win[10]), load_w8(win[16]))
    for tt in range(NTT):
        b2 = next_bank()

        def fnm(e, b2=b2, tt=tt):
            ins = None
            e.matmul(bank(b2), ones_bf[0:1, :], sgub_sb[0:1, :], start=True, stop=False)
            for h in range(4):
                pc = h // 2
                o = bank(b2)[:, h * 128:(h + 1) * 128]
                ins = e.matmul(o, vfall[:, tt, pc * 128:(pc + 1) * 128], wm[:, h * 128:(h + 1) * 128], start=False, stop=(h == 3))
            return ins
        S.add("pe", fnm, r=["vfall"] + [("wm", h) for h in range(4)], w=[("ps", b2)])
        for hh in range(2):
            ps_ = slice(hh * 64, hh * 64 + 64)
            src_ = bank(b2)[ps_, :].rearrange("p (c x) -> p c x", x=256)[:, :, hh * 128:(hh + 1) * 128]
            S.add("dve", lambda e, ps_=ps_, src_=src_, tt=tt: e.tensor_tensor(
                yT[ps_, 0:2, tt * 128:(tt + 1) * 128], src_, uT[ps_, :, tt * 128:(tt + 1) * 128], ALU.mult),
                r=[("ps", b2), ("uT", 0, tt // 4), ("uT", 1, tt // 4)], w=[("yT", tt, hh)])
    S.barrier()

    jobs = attention_jobs()
    vt_slices = [None] * 69
    for (ks, vt, _q, _m) in jobs:
        vt_slices[vt] = ks
    S.add("pool", lambda e: e.memset(vaug[:, :, 64:128], 1.0), w=["vaug_ones"])
    S.add("pool", lambda e: e.memset(qA[64:128, :], 0.0), w=["qAz"])
    S.add("pool", lambda e: e.memset(qB[0:64, :], 0.0), w=["qBz"])
    mask_ap = {"cp": MASK[:, 0:256], "c": MASK[:, 0:128], "ph": MASK[:, 256:384]}
    PBANKS = [0, 1, 2, 3, 4, 5]
    pb_ctr = [0]

    def nextpb():
        b = PBANKS[pb_ctr[0] % len(PBANKS)]
        pb_ctr[0] += 1
        return b

    rr_ctr = [0]
    groups = []
    curg, curn = [], 0
    for jb in jobs:
        n = jb[2][1]
        if curn + n > 512:
            groups.append(curg)
            curg, curn = [], 0
        curg.append(jb)
        curn += n
    if curg:
        groups.append(curg)
    mbuf = {}
    msrc = {"cp": (0, 256), "c": (0, 128), "ph": (256, 128)}
    for grp in groups:
        sig = tuple(jb[3] for jb in grp)
        if sig in mbuf:
            continue
        mb_ = view(alloc(512 * 2), BF16, 512)
        mbuf[sig] = mb_
        o = 0
        for mk in sig:
            m0, mn = msrc[mk]
            S.add("dve", lambda e, mb_=mb_, o=o, m0=m0, mn=mn: e.tensor_copy(mb_[:, o:o + mn], MASK[:, m0:m0 + mn]), w=["mbuf"])
            o += mn
    deferred = []

    def drain(n):
        for _ in range(n):
            if deferred:
                a_ = deferred.pop(0)
                S.add(a_[0], a_[1], r=a_[2], w=a_[3])
    for c in range(npairs if RUN_A3 else 0):
        if c in pre_w:
            sq_, sk_, sv_ = pre_w[c]
        else:
            sq_ = load_w8(win[4 + c])
            sk_ = load_w8(win[10 + c])
            sv_ = load_w8(win[16 + c])
        blocks = [("q", sq_, tb, NT + tb * 512) for tb in range(4)] + [("k", sk_, tb, tb * 512) for tb in range(8)]
        pbank = {}

        def add_proj(i):
            which, slot, tb, t0_ = blocks[i]
            tok = slice(t0_, t0_ + 512)
            b = nextpb()
            pbank[i] = b
            S.add("pe", mm_group(b, bank(b), [(w8[slot][:, kc, :], hnT[:, kc, tok]) for kc in range(8)]),
                  r=[("w8", slot)], w=[("ps", b)])
        add_proj(0)
        for i in range(len(blocks)):
            if i + 1 < len(blocks):
                add_proj(i + 1)
            which, slot, tb, t0_ = blocks[i]
            tok = slice(t0_, t0_ + 512)
            b = pbank[i]
            rs_ = rr_ctr[0] % 2
            rr_ctr[0] += 1
            S.add("act", lambda e, b=b, rs_=rs_: e.activation(raw[rs_], bank(b), AF.Copy),
                  r=[("ps", b)], w=[("raw", rs_)])
            b2 = nextpb()
            S.add("pe", lambda e, b2=b2, rs_=rs_: e.matmul(bank(b2), rotm, raw[rs_], start=True, stop=True),
                  r=[("raw", rs_)], w=[("ps", b2)])
            S.add("dve", lambda e, b2=b2, rs_=rs_, tok=tok: e.tensor_tensor(t1[rs_], bank(b2), sinT[:, tok], ALU.mult),
                  r=[("ps", b2)], w=[("t1", rs_)])
            S.add("pool", lambda e, rs_=rs_, tok=tok: e.tensor_tensor(t2[rs_], raw[rs_], cosT[:, tok], ALU.mult),
                  r=[("raw", rs_)], w=[("t2", rs_)])
            if which == "q":
                qs = slice(tb * 512, (tb + 1) * 512)
                S.add("dve", lambda e, rs_=rs_, qs=qs: e.tensor_tensor(qA[0:64, qs], t1[rs_][0:64, :], t2[rs_][0:64, :], ALU.add),
                      r=[("t1", rs_), ("t2", rs_)], w=[("qA", tb)])
                S.add("dve", lambda e, rs_=rs_, qs=qs: e.tensor_tensor(qB[64:128, qs], t1[rs_][64:128, :], t2[rs_][64:128, :], ALU.add),
                      r=[("t1", rs_), ("t2", rs_)], w=[("qB", tb)])
            else:
                S.add("dve", lambda e, rs_=rs_, tok=tok: e.tensor_tensor(kT[:, tok], t1[rs_], t2[rs_], ALU.add),
                      r=[("t1", rs_), ("t2", rs_)], w=[("kT", tb)])
            drain(1)
        for tb in range(8):
            tok = slice(tb * 512, (tb + 1) * 512)
            b = nextpb()
            S.add("pe", mm_group(b, bank(b), [(w8[sv_][:, kc, :], hnT[:, kc, tok]) for kc in range(8)]),
                  r=[("w8", sv_)], w=[("ps", b)])
            S.add("act", lambda e, b=b, tok=tok: e.activation(vraw[:, tok], bank(b), AF.Copy),
                  r=[("ps", b)], w=[("vraw", tb)])
        vraw_keys = [("vraw", tb) for tb in range(8)]
        for g0 in range(0, 69, 8):
            g1 = min(69, g0 + 8)
            b = nextpb()

            def fnt(e, b=b, g0=g0, g1=g1):
                ins = None
                for j, vt in enumerate(range(g0, g1)):
                    ins = e.transpose(bankbf(b)[:, j * 128:(j + 1) * 128], vraw[:, vt_slices[vt]], ident)
                return ins
            S.add("pe", fnt, r=vraw_keys, w=[("ps", b)])
            n = g1 - g0
            src = bankbf(b)[:, 0:n * 128].rearrange("p (t h d) -> p t h d", h=2, d=64)
            dst = vaug[:, g0:g1, :].rearrange("p t (h d) -> p t h d", d=64)[:, :, 0::2, :]
            S.add("dve", lambda e, src=src, dst=dst: e.tensor_copy(dst, src),
                  r=[("ps", b)], w=[("vaug", g0 // 8)])
        for hd in range(2):
            qpad = qA if hd == 0 else qB
            qkey = "qA" if hd == 0 else "qB"
            osl = slice(0, 64) if hd == 0 else slice(64, 128)
            dsl = slice(64, 128) if hd == 0 else slice(0, 64)
            vcols = slice(0, 128) if hd == 0 else slice(64, 192)
            G = len(groups)
            ginfo = []
            for gi, grp in enumerate(groups):
                offs = []
                o = 0
                for jb in grp:
                    offs.append(o)
                    o += jb[2][1]
                ginfo.append((grp, offs, o, 4 + (gi % 4), gi % NPT))

            def add_S(gi):
                grp, offs, ntot, sb_, pti = ginfo[gi]

                def fns(e, grp=grp, offs=offs, sb_=sb_, qpad=qpad):
                    ins = None
                    for jb, of in zip(grp, offs):
                        ks, vt, (q0, n, st), mk = jb
                        ins = e.matmul(bank(sb_)[:, of:of + n], kT[:, ks], qpad[:, q0:q0 + (n - 1) * st + 1:st], start=True, stop=True)
                    return ins
                S.add("pe", fns, r=[("kT", t) for t in range(8)] + [(qkey, t) for t in range(4)] + ["qAz", "qBz"],
                      w=[("ps", sb_)])
                S.add("act", lambda e, sb_=sb_, pti=pti, ntot=ntot: e.activation(PT[pti][:, 0:ntot], bank(sb_)[:, 0:ntot], AF.Exp, scale=0.125),
                      r=[("ps", sb_)], w=[("PT", pti)])
                sig = tuple(jb[3] for jb in grp)
                S.add("dve", lambda e, pti=pti, ntot=ntot, sig=sig: e.tensor_tensor(
                    PT[pti][:, 0:ntot], PT[pti][:, 0:ntot], mbuf[sig][:, 0:ntot], ALU.mult),
                    r=[("PT", pti), "mbuf"], w=[("PT", pti)])
                drain(1)

            def add_PV(gi):
                grp, offs, ntot, sb_, pti = ginfo[gi]

                def fnpv(e, grp=grp, offs=offs, pti=pti, vcols=vcols):
                    ins = None
                    for jb, of in zip(grp, offs):
                        ks, vt, (q0, n, st), mk = jb
                        for (i0, cnt, col) in out_pieces(q0, n, st):
                            ins = e.matmul(psall[:, col:col + (cnt - 1) * st + 1:st], vaug[:, vt, vcols],
                                           PT[pti][:, of + i0:of + i0 + cnt], start=False, stop=False, skip_group_check=True)
                    return ins
                banks_ = sorted({col // 512 for jb in grp for (_i0, _cnt, col) in out_pieces(*jb[2])})
                for b_ in banks_:
                    if b_ not in zeroed:
                        zeroed.add(b_)
                        S.add("pe", lambda e, b_=b_: e.matmul(bank(b_), zeros_bf, kT[:, 0:512], start=True, stop=False, skip_group_check=True),
                              r=[("kT", 0)], w=[("ps", b_)])
                S.add("pe", fnpv, r=[("PT", pti), "vaug_ones"] + [("vaug", g) for g in range(9)],
                      w=[("ps", b_) for b_ in banks_])

            zeroed = set()
            LA = 4
            for gi in range(G + LA):
                if gi < G:
                    add_S(gi)
                if gi >= LA:
                    add_PV(gi - LA)
            assert zeroed == {0, 1, 2, 3}
            vk = [("vraw", t) for t in range(8)]
            for b_ in range(4):
                cs = slice(b_ * 512, (b_ + 1) * 512)
                if b_ % 2 == 0:
                    S.add("act", lambda e, b_=b_, cs=cs: e.activation(Tsb[:, cs], bank(b_), AF.Copy),
                          r=[("ps", b_)], w=vk[2 * b_:2 * b_ + 2])
                else:
                    S.add("dve", lambda e, b_=b_, cs=cs: e.tensor_copy(Tsb[:, cs], bank(b_)),
                          r=[("ps", b_)], w=vk[2 * b_:2 * b_ + 2])
            for b_ in range(4):
                rb = b_ % 2
                cs = slice(b_ * 512, (b_ + 1) * 512)
                deferred.append(("act", lambda e, dsl=dsl, cs=cs: e.activation(Tsb[dsl, cs], Tsb[dsl, cs], AF.Ln),
                                 vk[2 * b_:2 * b_ + 2], vk[2 * b_:2 * b_ + 2]))
                deferred.append(("act", lambda e, rb=rb, osl=osl, dsl=dsl, cs=cs: e.activation(rec[rb][osl, :], Tsb[dsl, cs], AF.Exp, scale=-1.0),
                                 vk[2 * b_:2 * b_ + 2], [("rec", rb)]))
                deferred.append(("dve", lambda e, rb=rb, osl=osl, cs=cs, c=c: e.tensor_tensor(yT[osl, 2 + c, cs], Tsb[osl, cs], rec[rb][osl, :], ALU.mult),
                                 vk[2 * b_:2 * b_ + 2] + [("rec", rb)], [("yTb", c, hd, b_)]))
    drain(len(deferred))
    S.barrier()

    S.add("pool", lambda e: e.dma_start(out=wpp_sb, in_=wpp), w=["wpp"], dma="wpp")
    w22_ctr = [0]
    tmp_ctr = [0]
    pT_v = pT.rearrange("(c p) t -> p c t", p=128)
    out_v = outT.rearrange("(c p) t -> p c t", p=128)
    G1 = [(list(range(8)), 1024.0)]

    def hsrc(c, tb):
        return hT[:, c, tb * 512:(tb + 1) * 512]

    def hkeys(c, tb):
        return [("hT", c, tb)]

    def nbsrc(c, tb):
        return hnb[:, c, tb * 512:(tb + 1) * 512]

    def nbkeys(c, tb):
        return [("hnb", c, tb)]

    def ld_hT(half, tb):
        c0_ = NT + half * HT + tb * 512
        S.add("sp", lambda e, c0_=c0_, tb=tb: e.dma_start(out=hT[:, :, tb * 512:(tb + 1) * 512], in_=xv[:, :, c0_:c0_ + 512]),
              w=[("hT", o, tb) for o in range(8)], dma="hT%d" % tb)

    def ld_pT(half):
        h0 = half * HT
        S.add("pool", lambda e, h0=h0: e.dma_start(out=pTh, in_=pT_v[:, :, h0:h0 + HT]), w=["pTh"], dma="pTh")

    YG = [([0, 1], 256.0), ([2, 3, 4, 5, 6, 7], 768.0)]

    o_outE = o_aT + 8 * HT * 2
    outE = view(o_outE, F32, 8 * 512, 512)
    outE_keys = [("aT", f, t_) for f in range(8, 16) for t_ in range(2)]

    def norm_steps(kind, half, tb):
        h0 = half * HT
        if kind == "yn":
            src = lambda c, tb_, h0=h0: yT[:, c, h0 + tb_ * 512:h0 + (tb_ + 1) * 512]
            skeys = lambda c, tb_: []
            gcol, groups = 8, YG
            dst = lambda c, tb_: aT[:, c, tb_ * 512:(tb_ + 1) * 512]
            dkeys = lambda c, tb_: [("aT", c, tb_)]
        elif kind == "E":
            src, skeys, gcol, groups = hsrc, hkeys, 32, G1
            dst = lambda c, tb_: outE[:, c, :]
            dkeys = lambda c, tb_: [("aT", 8 + c, 0), ("aT", 8 + c, 1)]
        else:
            src, skeys, gcol, dst, dkeys, groups = hsrc, hkeys, (16 if kind == "nC" else 24), nbsrc, nbkeys, G1
        slot = (tb * len(groups)) % len(rstd2)
        flat = [(gi, c) for gi, (chunks, div) in enumerate(groups) for c in chunks]
        rinfo = {}
        for gi in range(len(groups)):
            si = (slot + gi) % len(rstd2)
            rinfo[gi] = (rstd2[si], ("rs", id(rstd2), si))

        def sq_ops(items):
            for (gi, c) in items:
                S.add("act", lambda e, c=c: e.activation(sq2[:, c, :], src(c, tb), AF.Square),
                      r=list(skeys(c, tb)), w=[("sqc2", c)])
                chunks, div = groups[gi]
                if c == chunks[-1]:
                    b = next_bank()
                    rr, rk = rinfo[gi]
                    S.add("pe", mm_group(b, bank(b), [(ones_bf, sq2[:, c_, :]) for c_ in chunks]),
                          r=[("sqc2", c_) for c_ in chunks] + ["cst"], w=[("ps", b)])
                    S.add("dve", lambda e, b=b, rr=rr, div=div: e.tensor_scalar(rr, bank(b), 1.0 / div, EPS, ALU.mult, ALU.add),
                          r=[("ps", b)], w=[rk])

        def nm_ops(items, first):
            if first:
                for gi in range(len(groups)):
                    rr, rk = rinfo[gi]
                    S.add("act", lambda e, rr=rr: e.activation(rr, rr, AF.Ln), r=[rk], w=[rk])
                    S.add("act", lambda e, rr=rr: e.activation(rr, rr, AF.Exp, scale=-0.5), r=[rk], w=[rk])
            for (gi, c) in items:
                rr, rk = rinfo[gi]
                S.add("dve", lambda e, c=c, rr=rr: e.scalar_tensor_tensor(
                    dst(c, tb), src(c, tb), gn[:, gcol + c:gcol + c + 1], rr, ALU.mult, ALU.mult),
                    r=list(skeys(c, tb)) + [rk, "gn"], w=list(dkeys(c, tb)))

        def p0():
            sq_ops(flat[:4])

        def p1():
            sq_ops(flat[4:])

        def p2():
            nm_ops(flat[:4], True)

        def p3():
            nm_ops(flat[4:], False)
            if kind == "E":
                S.add("sp", lambda e, h0=h0: e.dma_start(out=out_v[:, :, h0 + tb * 512:h0 + (tb + 1) * 512], in_=outE),
                      r=outE_keys, dma="out")
        return [p0, p1, p2, p3]

    def steps_B(tb):
        ts_ = slice(tb * 512, (tb + 1) * 512)

        def mk(o):
            def st():
                s_ = load_w8(wout[o])
                b = next_bank()
                S.add("pe", mm_group(b, bank(b), [(w8[s_][:, kc, :], aT[:, kc, ts_]) for kc in range(8)]),
                      r=[("w8", s_)] + [("aT", kc, tb) for kc in range(8)], w=[("ps", b)])
                S.add("dve", lambda e, b=b: e.tensor_tensor(hT[:, o, ts_], hT[:, o, ts_], bank(b), ALU.add),
                      r=[("ps", b), ("hT", o, tb)], w=[("hT", o, tb)])
            return st
        return [mk(o) for o in range(8)]

    def steps_GU(tb):
        ts_ = slice(tb * 512, (tb + 1) * 512)

        def mk(f):
            def st():
                sg = load_w8(wgate[f])
                su_ = load_w8(wup[f])
                bg = next_bank()
                S.add("pe", mm_group(bg, bank(bg), [(w8[sg][:, kc, :], hnb[:, kc, ts_]) for kc in range(8)]),
                      r=[("w8", sg)] + [("hnb", kc, tb) for kc in range(8)], w=[("ps", bg)])
                bu = next_bank()
                S.add("pe", mm_group(bu, bank(bu), [(w8[su_][:, kc, :], hnb[:, kc, ts_]) for kc in range(8)]),
                      r=[("w8", su_)] + [("hnb", kc, tb) for kc in range(8)], w=[("ps", bu)])
                tsl = tmp_ctr[0] % 2
                tmp_ctr[0] += 1
                S.add("act", lambda e, bg=bg, tsl=tsl: e.activation(tmpf[tsl], bank(bg), AF.Silu),
                      r=[("ps", bg)], w=[("tmpf", tsl)])
                S.add("dve", lambda e, bu=bu, tsl=tsl: e.tensor_tensor(aT[:, f, ts_], tmpf[tsl], bank(bu), ALU.mult),
                      r=[("ps", bu), ("tmpf", tsl)], w=[("aT", f, tb)])
            return st
        return [mk(f) for f in range(22)]

    def steps_DN(tb):
        ts_ = slice(tb * 512, (tb + 1) * 512)

        def mk(o):
            def st():
                b = next_bank()
                for hf in range(2):
                    s_ = w22_ctr[0] % 4
                    w22_ctr[0] += 1
                    S.add("pool", lambda e, s_=s_, hf=hf: e.dma_start(out=w22h[s_], in_=wdown[o][:, hf * 11:(hf + 1) * 11, :]),
                          w=[("w22h", s_)], dma="w22h_%d" % s_)

                    def fn(e, s_=s_, hf=hf, b=b):
                        ins = None
                        for j in range(11):
                            f = hf * 11 + j
                            ins = e.matmul(bank(b), w22h[s_][:, j, :], aT[:, f, ts_], start=(f == 0), stop=(f == 21))
                        return ins
                    S.add("pe", fn, r=[("w22h", s_)] + [("aT", f, tb) for f in range(hf * 11, hf * 11 + 11)], w=[("ps", b)])
                S.add("dve", lambda e, b=b: e.tensor_tensor(hT[:, o, ts_], hT[:, o, ts_], bank(b), ALU.add),
                      r=[("ps", b), ("hT", o, tb)], w=[("hT", o, tb)])
            return st
        return [mk(o) for o in range(8)]

    def steps_D(tb):
        ts_ = slice(tb * 512, (tb + 1) * 512)

        def mk(o):
            def st():
                s_ = load_w8(wpg[o])
                bg = next_bank()
                S.add("pe", mm_group(bg, bank(bg), [(w8[s_][:, kc, :], hnb[:, kc, ts_]) for kc in range(8)]),
                      r=[("w8", s_)] + [("hnb", kc, tb) for kc in range(8)], w=[("ps", bg)])
                bp = next_bank()
                S.add("pe", mm_group(bp, bank(bp), [(wpp_sb[:, kc, o * 128:(o + 1) * 128], pTh[:, kc, ts_]) for kc in range(2)]),
                      r=["wpp", "pTh"], w=[("ps", bp)])
                tsl = tmp_ctr[0] % 2
                tmp_ctr[0] += 1
                S.add("act", lambda e, bg=bg, tsl=tsl: e.activation(tmpf[tsl], bank(bg), AF.Sigmoid),
                      r=[("ps", bg)], w=[("tmpf", tsl)])
                S.add("dve", lambda e, bp=bp, tsl=tsl: e.tensor_tensor(tmpf[tsl], tmpf[tsl], bank(bp), ALU.mult),
                      r=[("ps", bp), ("tmpf", tsl)], w=[("tmpf", tsl)])
                S.add("dve", lambda e, tsl=tsl: e.tensor_tensor(hT[:, o, ts_], hT[:, o, ts_], tmpf[tsl], ALU.add),
                      r=[("tmpf", tsl), ("hT", o, tb)], w=[("hT", o, tb)])
            return st
        return [mk(o) for o in range(8)]

    def run(steps, inserts=None):
        inserts = inserts or {}
        for i, st in enumerate(steps):
            st()
            for x in inserts.get(i, ()):
                x()

    def spread(parts, first):
        return {first + i: [p] for i, p in enumerate(parts)}

    def merge(*ds):
        out = {}
        for d in ds:
            for k, v in d.items():
                out.setdefault(k, []).extend(v)
        return out

    if RUN_POST:
        N = norm_steps
        ld_hT(0, 0); ld_hT(0, 1); ld_pT(0)
        for tb in range(2):
            for p in N("yn", 0, tb):
                p()
        for half in range(2):
            nC0 = N("nC", half, 0); nC1 = N("nC", half, 1)
            nD0 = N("nD", half, 0); nD1 = N("nD", half, 1)
            E0 = N("E", half, 0); E1 = N("E", half, 1)
            if half == 0:
                run(steps_B(0))
            run(steps_B(1), spread(nC0, 1))
            run(steps_GU(0), spread(nC1, 1))
            run(steps_GU(1))
            run(steps_DN(0))
            run(steps_DN(1), spread(nD0, 1))
            if half == 0:
                yn0 = N("yn", 1, 0); yn1 = N("yn", 1, 1)
                run(steps_D(0), merge(spread(nD1, 0), spread(yn0, 4)))
                run(steps_D(1), merge(spread(E0, 0), {3: [lambda: ld_hT(1, 0)]}, spread(yn1, 4)))
                run(steps_B(0), merge(spread(E1, 0), {3: [lambda: ld_hT(1, 1), lambda: ld_pT(1)]}))
            else:
                run(steps_D(0), spread(nD1, 0))
                run(steps_D(1), spread(E0, 1))
                for p in E1:
                    p()
    S.barrier()
    if dbg is not None:
        dbuf = debug[0](locals())
        S.add("pool", lambda e: e.dma_start(out=dbg, in_=dbuf), dma="dbg")
        S.barrier()
    S.finalize()

    sems = {}
    for e in ENGS:
        sems[("eng", e)] = es.enter_context(nc.semaphore("s_" + e))
    for k in S.dma_cnt:
        sems[("dma", k)] = es.enter_context(nc.semaphore("d_" + k))
    with es:
        with nc.Block() as block:
            @block.tensor
            def _(t):
                S.emit("pe", t, sems)

            @block.scalar
            def _(s):
                S.emit("act", s, sems)

            @block.vector
            def _(v):
                S.emit("dve", v, sems)

            @block.gpsimd
            def _(g):
                S.emit("pool", g, sems)

            @block.sync
            def _(sy):
                S.emit("sp", sy, sems)
                for k, cnt in S.dma_cnt.items():
                    sy.wait_ge(sems[("dma", k)], 16 * cnt)
    return nc


def _tile_w(w, kc):
    K, N = w.shape
    return np.ascontiguousarray(w.reshape(kc, 128, N // 128, 128).transpose(2, 1, 0, 3))


def prep_inputs(x, p, mix_norm_g, w_in, sgu_w, sgu_b, sgu_norm_g, out_norm_a, out_norm_b,
                w_out, ffn_norm_g, w_gate, w_up, w_down, ple_norm_g, w_ple_gate,
                w_ple_proj, final_norm_g):
    f32 = np.float32
    x = np.asarray(x, f32); p = np.asarray(p, f32)
    shared = {
        "win": _tile_w(np.asarray(w_in[0], f32), 8),
        "wout": _tile_w(np.asarray(w_out[0], f32), 8),
        "wgate": _tile_w(np.asarray(w_gate[0], f32), 8),
        "wup": _tile_w(np.asarray(w_up[0], f32), 8),
        "wdown": _tile_w(np.asarray(w_down[0], f32), 22),
        "wpg": _tile_w(np.asarray(w_ple_gate[0], f32), 8),
        "wpp": np.ascontiguousarray(np.asarray(w_ple_proj[0], f32).reshape(2, 128, 1024).transpose(1, 0, 2)),
    }
    gcols = np.concatenate([
        np.asarray(mix_norm_g[0], f32).reshape(8, 128),
        np.concatenate([np.asarray(out_norm_a[0], f32), np.asarray(out_norm_b[0], f32)]).reshape(8, 128),
        np.asarray(ffn_norm_g[0], f32).reshape(8, 128),
        np.asarray(ple_norm_g[0], f32).reshape(8, 128),
        np.asarray(final_norm_g, f32).reshape(8, 128)], axis=0)
    shared["gains"] = np.ascontiguousarray(gcols.T)
    shared["sgug"] = np.ascontiguousarray(np.broadcast_to(np.asarray(sgu_norm_g[0], f32)[None, :], (128, 256)))
    shared["sguw"] = np.ascontiguousarray(np.asarray(sgu_w[0], f32).transpose(2, 0, 1).reshape(128, 512))
    shared["sgub"] = np.ascontiguousarray(np.asarray(sgu_b[0], f32).reshape(1, 512))
    ii = np.arange(128)
    ident = np.eye(128, dtype=f32)
    rotm = np.zeros((128, 128), f32)
    for m in range(128):
        if (m % 64) < 32:
            rotm[m + 32, m] = -1.0
        else:
            rotm[m - 32, m] = 1.0
    mcur = (ii[None, :] >= ii[:, None]).astype(f32)
    mprev = (ii[None, :] <= ii[:, None]).astype(f32)
    inv = (10000.0 ** (-np.arange(32, dtype=f32) / f32(32))).astype(f32)
    in_maps = []
    for core in range(NCORES):
        b, nci = core // 4, core % 4
        own = x[b, nci * NT:(nci + 1) * NT, :]
        if nci > 0:
            halo = x[b, (nci - 1) * NT:nci * NT, :]
        else:
            halo = np.zeros_like(own)
        xall = np.ascontiguousarray(np.concatenate([halo, own], axis=0).T)
        pTc = np.ascontiguousarray(p[0, b, nci * NT:(nci + 1) * NT, :].T)
        mph = mprev if nci > 0 else np.zeros_like(mprev)
        cstc = np.concatenate([ident, rotm, mcur, mprev * 0 + mprev, mph, np.ones((128, 128), f32), np.zeros((128, 128), f32)], axis=1)
        cstc[:, 384:512] = mprev
        pos = (np.arange(NA, dtype=np.int64) + (nci - 1) * NT).astype(f32)
        ang = pos[None, :] * inv[:, None]
        cs = np.cos(ang).astype(f32); sn = np.sin(ang).astype(f32)
        ropec = np.stack([np.tile(cs, (4, 1)), np.tile(sn, (4, 1))], axis=1)
        m = dict(shared)
        m["xall"] = xall
        m["pT"] = pTc
        m["cst"] = np.ascontiguousarray(cstc)
        m["rope"] = np.ascontiguousarray(ropec.astype(f32))
        in_maps.append(m)
    return in_maps


_NC_CACHE = {}


def kernel(**inputs):
    in_maps = prep_inputs(**inputs)
    if "nc" not in _NC_CACHE:
        _NC_CACHE["nc"] = build_nc()
    nc = _NC_CACHE["nc"]
    res = run_bass_kernel_spmd(nc, in_maps, core_ids=list(range(NCORES)))
    out = np.empty((2, 8192, D), np.float32)
    for core in range(NCORES):
        b, nci = core // 4, core % 4
        out[b, nci * NT:(nci + 1) * NT, :] = np.asarray(res.results[core]["outT"]).T
    return out
```
